# Optimizing a Trainium2 kernel written in Bass

```python
import math, functools
import jax, jax.numpy as jnp
from jax import lax
import numpy as np


D_MODEL = 2048
BATCH = 4
SEQ = 4096
DEPTH = 4

N_MIXERS = 3
GRID_W = 64
ROPE_BASE = 10000.0
NORM_EPS = 1e-6
GN_EPS = 1e-5

RET_HEADS = 8
RET_DK = D_MODEL // RET_HEADS
RET_DV = 2 * RET_DK
RET_CHUNK = 128

NA_HEADS = D_MODEL // 128
NA_DH = D_MODEL // NA_HEADS
NA_WIN_R = 8
NA_WIN_C = 16

MLA_HEADS = D_MODEL // 128
MLA_Q_RANK = 512
MLA_KV_RANK = 512
MLA_NOPE = 128
MLA_ROPE = 64
MLA_V = 128
MLA_QBLOCK = 128

D_FF = 11 * D_MODEL // 4
CONV_W = 3

kernel_name = "hybrid_retention_natten_mla_convffn_encoder"


def rmsnorm(x, g):
    xf = x.astype(jnp.float32)
    y = xf * lax.rsqrt(jnp.mean(jnp.square(xf), axis=-1, keepdims=True) + NORM_EPS)
    return (y * g.astype(jnp.float32)).astype(x.dtype)


def rope(x, pos):
    d = x.shape[-1]
    half = d // 2
    inv = ROPE_BASE ** (-jnp.arange(half, dtype=jnp.float32) * 2.0 / d)
    ang = pos[:, None] * inv[None, :]
    cos, sin = jnp.cos(ang), jnp.sin(ang)
    xf = x.astype(jnp.float32)
    x1, x2 = xf[..., :half], xf[..., half:]
    return jnp.concatenate([x1 * cos - x2 * sin, x1 * sin + x2 * cos], axis=-1).astype(x.dtype)


def retention_dir(q, k, v, log_gamma, strict):
    B, H, T, dk = q.shape
    dv = v.shape[-1]
    C = RET_CHUNK
    N = T // C
    idx = jnp.arange(C, dtype=jnp.float32)
    diff = idx[:, None] - idx[None, :]
    mask = (diff > 0) if strict else (diff >= 0)
    lg = log_gamma[:, None, None]
    intra_decay = jnp.where(mask, jnp.exp(jnp.where(mask, diff, 0.0) * lg), 0.0)
    q_decay = jnp.exp((idx + 1.0) * log_gamma[:, None])[..., None]
    k_decay = jnp.exp((C - 1.0 - idx) * log_gamma[:, None])[..., None]
    chunk_decay = jnp.exp(C * log_gamma)[:, None, None]

    def to_chunks(a):
        return jnp.moveaxis(a.reshape(B, H, N, C, a.shape[-1]), 2, 0)

    def step(state, qkv):
        qc, kc, vc = qkv
        scores = jnp.einsum('bhid,bhjd->bhij', qc, kc) * intra_decay
        out = (jnp.einsum('bhij,bhjv->bhiv', scores, vc)
               + jnp.einsum('bhid,bhdv->bhiv', qc * q_decay, state))
        state = state * chunk_decay + jnp.einsum('bhjd,bhjv->bhdv', kc * k_decay, vc)
        return state, out

    state0 = jnp.zeros((B, H, dk, dv), jnp.float32)
    _, out = lax.scan(step, state0, (to_chunks(q), to_chunks(k), to_chunks(v)))
    return jnp.moveaxis(out, 0, 2).reshape(B, H, T, dv)


def retention_mixer(h, w_in, decay_fwd, decay_bwd, w_out, pos):
    B, T, _ = h.shape
    qk_w = RET_HEADS * RET_DK
    v_w = RET_HEADS * RET_DV
    proj = h @ w_in

    def heads(a, d):
        return a.reshape(B, T, RET_HEADS, d).transpose(0, 2, 1, 3).astype(jnp.float32)

    q = rope(heads(proj[..., :qk_w], RET_DK), pos)
    k = rope(heads(proj[..., qk_w:2 * qk_w], RET_DK), pos) * (RET_DK ** -0.5)
    v = heads(proj[..., 2 * qk_w:2 * qk_w + v_w], RET_DV)
    g = proj[..., 2 * qk_w + v_w:]
    lg_f = jax.nn.log_sigmoid(decay_fwd.astype(jnp.float32))
    lg_b = jax.nn.log_sigmoid(decay_bwd.astype(jnp.float32))
    o_f = retention_dir(q, k, v, lg_f, False)
    flip = lambda a: jnp.flip(a, axis=2)
    o_b = flip(retention_dir(flip(q), flip(k), flip(v), lg_b, True))
    o = o_f + o_b
    mu = jnp.mean(o, axis=-1, keepdims=True)
    var = jnp.mean(jnp.square(o - mu), axis=-1, keepdims=True)
    o = (o - mu) * lax.rsqrt(var + GN_EPS)
    o = o.transpose(0, 2, 1, 3).reshape(B, T, v_w)
    y = (jax.nn.silu(g.astype(jnp.float32)) * o).astype(h.dtype)
    return y @ w_out


def neighbourhood_attention(h, w_qkv, rpb, w_out):
    B, T, D = h.shape
    rows = T // GRID_W
    wr = min(NA_WIN_R, rows)
    qkv = (h @ w_qkv).reshape(B, rows, GRID_W, 3, NA_HEADS, NA_DH)
    qkv = jnp.transpose(qkv, (3, 0, 4, 1, 2, 5))
    q, k, v = qkv[0] * (NA_DH ** -0.5), qkv[1], qkv[2]

    col = jnp.arange(GRID_W)
    c_start = jnp.clip(col - NA_WIN_C // 2, 0, GRID_W - NA_WIN_C)
    col_mask = (col[None, :] >= c_start[:, None]) & (col[None, :] < c_start[:, None] + NA_WIN_C)
    dc_idx = jnp.clip(col[None, :] - col[:, None] + NA_WIN_C - 1, 0, 2 * NA_WIN_C - 2)

    def row_block(r):
        r_start = jnp.clip(r - wr // 2, 0, rows - wr)
        dr_idx = r_start + jnp.arange(wr) - r + NA_WIN_R - 1
        bias = rpb[:, dr_idx[:, None, None], dc_idx[None]]
        bias = jnp.transpose(bias, (0, 2, 1, 3)).astype(jnp.float32)
        q_r = lax.dynamic_index_in_dim(q, r, axis=2, keepdims=False)
        k_b = lax.dynamic_slice_in_dim(k, r_start, wr, axis=2)
        v_b = lax.dynamic_slice_in_dim(v, r_start, wr, axis=2)
        s = jnp.einsum('bhqd,bhrkd->bhqrk', q_r, k_b).astype(jnp.float32) + bias
        s = jnp.where(col_mask[:, None, :], s, -jnp.inf)
        p = jax.nn.softmax(s.reshape(B, NA_HEADS, GRID_W, wr * GRID_W), axis=-1).reshape(s.shape)
        return jnp.einsum('bhqrk,bhrkd->bhqd', p.astype(v_b.dtype), v_b)

    o = lax.map(row_block, jnp.arange(rows))
    o = jnp.transpose(o, (1, 0, 3, 2, 4)).reshape(B, T, NA_HEADS * NA_DH)
    return o @ w_out


def mla(h, w_down, q_norm, kv_norm, w_uq, w_ukv, w_out, pos):
    B, T, _ = h.shape
    H = MLA_HEADS
    down = h @ w_down
    c_q = rmsnorm(down[..., :MLA_Q_RANK], q_norm)
    c_kv = rmsnorm(down[..., MLA_Q_RANK:MLA_Q_RANK + MLA_KV_RANK], kv_norm)
    k_rope = down[..., MLA_Q_RANK + MLA_KV_RANK:]
    q = (c_q @ w_uq).reshape(B, T, H, MLA_NOPE + MLA_ROPE).transpose(0, 2, 1, 3)
    q = jnp.concatenate([q[..., :MLA_NOPE], rope(q[..., MLA_NOPE:], pos)], axis=-1)
    q = q * ((MLA_NOPE + MLA_ROPE) ** -0.5)
    kv = (c_kv @ w_ukv).reshape(B, T, H, MLA_NOPE + MLA_V).transpose(0, 2, 1, 3)
    k_rope = rope(k_rope[:, None], pos)
    k = jnp.concatenate([kv[..., :MLA_NOPE], jnp.broadcast_to(k_rope, (B, H, T, MLA_ROPE))], axis=-1)
    v = kv[..., MLA_NOPE:]
    nb = T // MLA_QBLOCK
    q_blocks = jnp.moveaxis(q.reshape(B, H, nb, MLA_QBLOCK, MLA_NOPE + MLA_ROPE), 2, 0)

    def attend(q_blk):
        s = jnp.einsum('bhqd,bhkd->bhqk', q_blk, k).astype(jnp.float32)
        p = jax.nn.softmax(s, axis=-1)
        return jnp.einsum('bhqk,bhkd->bhqd', p.astype(v.dtype), v)

    o = lax.map(attend, q_blocks)
    o = jnp.moveaxis(o, 0, 2).reshape(B, H, T, MLA_V).transpose(0, 2, 1, 3).reshape(B, T, H * MLA_V)
    return o @ w_out


def conv_ffn(h, w_up, conv_w, conv_b, w_down):
    u = h @ w_up
    u = lax.conv_general_dilated(
        u, conv_w[:, None, :], window_strides=(1,), padding=((CONV_W // 2, CONV_W // 2),),
        dimension_numbers=('NWC', 'WIO', 'NWC'), feature_group_count=2 * D_FF) + conv_b
    g, val = u[..., :D_FF], u[..., D_FF:]
    return (jax.nn.silu(g) * val) @ w_down


def setup_inputs(seed: int = 0) -> dict:
    key = jax.random.key(seed)
    keys = iter(jax.random.split(key, 64))
    f32 = jnp.float32

    def dense(shape, fan_in):
        return jax.random.normal(next(keys), shape, f32) * (fan_in ** -0.5)

    def gain(n):
        return 1.0 + 0.02 * jax.random.normal(next(keys), (n,), f32)

    def small(shape, s):
        return s * jax.random.normal(next(keys), shape, f32)

    def decay_logits():
        a = 5.0 + jnp.arange(RET_HEADS, dtype=f32)
        return jnp.log(jnp.exp2(a) - 1.0) + small((RET_HEADS,), 0.05)

    ret_in = 2 * RET_HEADS * RET_DK + 2 * RET_HEADS * RET_DV
    p = {}
    p["x"] = jax.random.normal(next(keys), (BATCH, SEQ, D_MODEL), f32)

    def add_ffn(pre):
        p[pre + "ffn_norm"] = gain(D_MODEL)
        p[pre + "ffn_w_up"] = dense((D_MODEL, 2 * D_FF), D_MODEL)
        p[pre + "ffn_conv_w"] = dense((CONV_W, 2 * D_FF), CONV_W)
        p[pre + "ffn_conv_b"] = small((2 * D_FF,), 0.01)
        p[pre + "ffn_w_down"] = dense((D_FF, D_MODEL), D_FF)

    def add_ret(pre):
        p[pre + "attn_norm"] = gain(D_MODEL)
        p[pre + "ret_w_in"] = dense((D_MODEL, ret_in), D_MODEL)
        p[pre + "ret_decay_fwd"] = decay_logits()
        p[pre + "ret_decay_bwd"] = decay_logits()
        p[pre + "ret_w_out"] = dense((RET_HEADS * RET_DV, D_MODEL), RET_HEADS * RET_DV)

    add_ret("l0_")
    add_ffn("l0_")
    p["l1_attn_norm"] = gain(D_MODEL)
    p["l1_na_w_qkv"] = dense((D_MODEL, 3 * NA_HEADS * NA_DH), D_MODEL)
    p["l1_na_rpb"] = small((NA_HEADS, 2 * NA_WIN_R - 1, 2 * NA_WIN_C - 1), 0.1)
    p["l1_na_w_out"] = dense((NA_HEADS * NA_DH, D_MODEL), NA_HEADS * NA_DH)
    add_ffn("l1_")
    p["l2_attn_norm"] = gain(D_MODEL)
    p["l2_mla_w_down"] = dense((D_MODEL, MLA_Q_RANK + MLA_KV_RANK + MLA_ROPE), D_MODEL)
    p["l2_mla_q_norm"] = gain(MLA_Q_RANK)
    p["l2_mla_kv_norm"] = gain(MLA_KV_RANK)
    p["l2_mla_w_uq"] = dense((MLA_Q_RANK, MLA_HEADS * (MLA_NOPE + MLA_ROPE)), MLA_Q_RANK)
    p["l2_mla_w_ukv"] = dense((MLA_KV_RANK, MLA_HEADS * (MLA_NOPE + MLA_V)), MLA_KV_RANK)
    p["l2_mla_w_out"] = dense((MLA_HEADS * MLA_V, D_MODEL), MLA_HEADS * MLA_V)
    add_ffn("l2_")
    add_ret("l3_")
    add_ffn("l3_")
    p["final_norm"] = gain(D_MODEL)
    return p


def reference(x,
              l0_attn_norm, l0_ret_w_in, l0_ret_decay_fwd, l0_ret_decay_bwd, l0_ret_w_out,
              l0_ffn_norm, l0_ffn_w_up, l0_ffn_conv_w, l0_ffn_conv_b, l0_ffn_w_down,
              l1_attn_norm, l1_na_w_qkv, l1_na_rpb, l1_na_w_out,
              l1_ffn_norm, l1_ffn_w_up, l1_ffn_conv_w, l1_ffn_conv_b, l1_ffn_w_down,
              l2_attn_norm, l2_mla_w_down, l2_mla_q_norm, l2_mla_kv_norm, l2_mla_w_uq, l2_mla_w_ukv, l2_mla_w_out,
              l2_ffn_norm, l2_ffn_w_up, l2_ffn_conv_w, l2_ffn_conv_b, l2_ffn_w_down,
              l3_attn_norm, l3_ret_w_in, l3_ret_decay_fwd, l3_ret_decay_bwd, l3_ret_w_out,
              l3_ffn_norm, l3_ffn_w_up, l3_ffn_conv_w, l3_ffn_conv_b, l3_ffn_w_down,
              final_norm):
    T = x.shape[1]
    pos = jnp.arange(T, dtype=jnp.float32)
    attn_norms = [l0_attn_norm, l1_attn_norm, l2_attn_norm, l3_attn_norm]
    mixer_params = [
        (l0_ret_w_in, l0_ret_decay_fwd, l0_ret_decay_bwd, l0_ret_w_out),
        (l1_na_w_qkv, l1_na_rpb, l1_na_w_out),
        (l2_mla_w_down, l2_mla_q_norm, l2_mla_kv_norm, l2_mla_w_uq, l2_mla_w_ukv, l2_mla_w_out),
        (l3_ret_w_in, l3_ret_decay_fwd, l3_ret_decay_bwd, l3_ret_w_out),
    ]
    ffn_norms = [l0_ffn_norm, l1_ffn_norm, l2_ffn_norm, l3_ffn_norm]
    ffn_params = [
        (l0_ffn_w_up, l0_ffn_conv_w, l0_ffn_conv_b, l0_ffn_w_down),
        (l1_ffn_w_up, l1_ffn_conv_w, l1_ffn_conv_b, l1_ffn_w_down),
        (l2_ffn_w_up, l2_ffn_conv_w, l2_ffn_conv_b, l2_ffn_w_down),
        (l3_ffn_w_up, l3_ffn_conv_w, l3_ffn_conv_b, l3_ffn_w_down),
    ]
    h = x
    for i in range(DEPTH):
        a = rmsnorm(h, attn_norms[i])
        kind = i % N_MIXERS
        if kind == 0:
            mix = retention_mixer(a, *mixer_params[i], pos)
        elif kind == 1:
            mix = neighbourhood_attention(a, *mixer_params[i])
        else:
            mix = mla(a, *mixer_params[i], pos)
        h = h + mix.astype(h.dtype)
        h = h + conv_ffn(rmsnorm(h, ffn_norms[i]), *ffn_params[i]).astype(h.dtype)
    return rmsnorm(h, final_norm)
```

```python
import contextlib
import math
import numpy as np
import concourse.bass as bass
import concourse.mybir as mybir
from concourse.bass_utils import run_bass_kernel_spmd

F32 = mybir.dt.float32
BF16 = mybir.dt.bfloat16
AF = mybir.ActivationFunctionType
ALU = mybir.AluOpType
AX = mybir.AxisListType

ENGS = ("pe", "act", "dve", "pool", "sp")
NDSEM = 48
PAIRS = [[0, 1], [2, 3], [4, 5], [6, 7]]

D = 2048
TL = 2048
SEQ = 4096
DC = D // 128
NT = TL // 512
DFF = 5632
FC = DFF // 128
RH = 8


class Buf:
    __slots__ = ("w", "r", "name")

    def __init__(self, name=""):
        self.w = None
        self.r = []
        self.name = name


class Prog:
    def __init__(self, nc, stack):
        self.nc = nc
        self.ops = {e: [] for e in ENGS}
        self.cnt = {e: 0 for e in ENGS}
        self.sem = {e: stack.enter_context(nc.semaphore("s_" + e)) for e in ENGS}
        self.dsem = [stack.enter_context(nc.semaphore("d%d" % i)) for i in range(NDSEM)]
        self.dcnt = [0] * NDSEM
        self.dnext = 0
        self.ccsem = stack.enter_context(nc.semaphore("ccsem"))
        self.cccnt = 0
        self.seen = {e: {} for e in ENGS}
        self.nwaits = 0
        self.nops = 0
        self.pairs = PAIRS

    def _semof(self, key):
        if key[0] == "d":
            return self.dsem[key[1]]
        if key[0] == "cc":
            return self.ccsem
        return self.sem[key[0]]

    def _wait(self, eng, tok):
        key, val = tok
        if key[0] == eng and eng == "pe":
            return
        if self.seen[eng].get(key, 0) >= val:
            return
        self.seen[eng][key] = val
        s = self._semof(key)
        self.ops[eng].append(lambda E, s=s, val=val: E.wait_ge(s, val))
        self.nwaits += 1

    def _deps(self, eng, reads, writes):
        for b in reads:
            if b.w is not None:
                self._wait(eng, b.w)
        for b in writes:
            if b.w is not None:
                self._wait(eng, b.w)
            for t in b.r:
                self._wait(eng, t)

    def _commit(self, tok, reads, writes):
        for b in reads:
            b.r.append(tok)
        for b in writes:
            b.w = tok
            b.r = []

    def op(self, eng, fn, reads=(), writes=(), inc=True):
        self._deps(eng, reads, writes)
        self.nops += 1
        if not inc:
            self.ops[eng].append(lambda E, fn=fn: fn(E))
            tok = ((eng,), self.cnt[eng] + 1)
            self._commit(tok, reads, writes)
            return tok
        self.cnt[eng] += 1
        n = self.cnt[eng]
        s = self.sem[eng]
        self.ops[eng].append(lambda E, fn=fn, s=s: fn(E).then_inc(s, 1))
        tok = ((eng,), n)
        self._commit(tok, reads, writes)
        return tok

    def dma(self, q, out, in_, reads=(), writes=(), **kw):
        self._deps(q, reads, writes)
        i = self.dnext
        self.dnext = (self.dnext + 1) % NDSEM
        if self.dcnt[i] > 0:
            self._wait(q, (("d", i), self.dcnt[i]))
        self.dcnt[i] += 16
        s = self.dsem[i]
        self.ops[q].append(lambda E, s=s, out=out, in_=in_, kw=kw:
                           E.dma_start(out=out, in_=in_, **kw).then_inc(s, 16))
        tok = (("d", i), self.dcnt[i])
        self._commit(tok, reads, writes)
        self.nops += 1
        return tok

    def allgather(self, in_ap, out_ap, reads=(), writes=(), allranks=False):
        self._deps("pool", reads, writes)
        self.cccnt += 1
        s = self.ccsem
        groups = self.allgroup if allranks else self.pairs
        self.ops["pool"].append(lambda E, s=s, in_ap=in_ap, out_ap=out_ap, groups=groups: E.collective_compute(
            "AllGather", ALU.bypass, replica_groups=groups,
            ins=[in_ap.opt()], outs=[out_ap.opt()]).then_inc(s, 1))
        tok = (("cc",), self.cccnt)
        self._commit(tok, reads, writes)
        return tok

    def barrier(self):
        toks = [((e,), self.cnt[e]) for e in ENGS if self.cnt[e] > 0]
        toks += [(("d", i), self.dcnt[i]) for i in range(NDSEM) if self.dcnt[i] > 0]
        if self.cccnt:
            toks.append((("cc",), self.cccnt))
        for e in ENGS:
            for t in toks:
                if t[0][0] == e:
                    if e == "pe":
                        continue
                self._wait(e, t)

    def emit(self):
        nc = self.nc
        ops = self.ops
        with nc.Block() as block:
            @block.tensor
            def _(E):
                for f in ops["pe"]:
                    f(E)

            @block.scalar
            def _(E):
                for f in ops["act"]:
                    f(E)

            @block.vector
            def _(E):
                for f in ops["dve"]:
                    f(E)

            @block.gpsimd
            def _(E):
                for f in ops["pool"]:
                    f(E)

            @block.sync
            def _(E):
                for f in ops["sp"]:
                    f(E)
        self.ops = {e: [] for e in ENGS}


class _Stop(Exception):
    pass


class KB:
    def cstop(self, k):
        if self.cfg.get("corestop") == k:
            raise _Stop()

    def __init__(self, cfg):
        self.cfg = cfg
        self.nc = bass.Bass("TRN2", target_bir_lowering=False)
        self.uid = 0

    def din(self, name, shape, dt=F32):
        if not hasattr(self, "ext_names"):
            self.ext_names = []
        self.ext_names.append(name)
        return self.nc.dram_tensor(name, list(shape), dt, kind="ExternalInput").ap()

    def dscratch(self, name, shape, dt):
        return self.nc.dram_tensor(name, list(shape), dt).ap()

    def sb(self, st, name, shape, dt):
        if st is None:
            raise _Stop()
        self.uid += 1
        return st.enter_context(self.nc.sbuf_tensor("%s_%d" % (name, self.uid), list(shape), dt))

    def end_stage(self):
        self.P.barrier()
        self.P.emit()

    def build(self):
        nc = self.nc
        cfg = self.cfg
        with contextlib.ExitStack() as top:
            self.P = P = Prog(nc, top)
            P.pairs = [[2 * i, 2 * i + 1] for i in range(cfg.get("ncores", 8) // 2)]
            P.allgroup = [list(range(cfg.get("ncores", 8)))]
            self.declare_io()
            self.ps = [top.enter_context(nc.psum_tensor("psf%d" % i, [128, 512], F32)) for i in range(6)]
            self.psb = [Buf() for _ in range(6)]
            self.pb = [top.enter_context(nc.psum_tensor("psb%d" % i, [128, 1024], BF16)) for i in range(2)]
            self.pbb = [Buf() for _ in range(2)]
            self.psi = 0
            self.pbi = 0
            cs = self.consts = {}
            cb = self.cbuf = Buf("consts")
            cs["gains"] = self.sb(top, "gains", [128, 9, DC], F32)
            cs["ident_f"] = self.sb(top, "ident_f", [128, 128], F32)
            cs["ident"] = self.sb(top, "ident", [128, 128], BF16)
            cs["ones"] = self.sb(top, "ones", [128, 128], F32)
            cs["onesb"] = self.sb(top, "onesb", [128, 128], BF16)
            cs["m01"] = self.sb(top, "m01", [128, 2], F32)
            cs["eps6"] = self.sb(top, "eps6", [128, 1], F32)
            cs["eps5"] = self.sb(top, "eps5", [128, 1], F32)
            cs["one1"] = self.sb(top, "one1", [128, 1], F32)
            P.dma("sp", cs["gains"][:], self.io["gains"], writes=[cb])
            P.dma("sp", cs["ident_f"][:], self.io["ident"], writes=[cb])
            P.dma("sp", cs["m01"][:], self.io["m01"], writes=[cb])
            P.op("pool", lambda E: E.memset(cs["ones"][:], 1.0), writes=[cb])
            P.op("pool", lambda E: E.memset(cs["onesb"][:], 1.0), writes=[cb])
            P.op("pool", lambda E: E.memset(cs["eps6"][:], 1e-6), writes=[cb])
            P.op("pool", lambda E: E.memset(cs["eps5"][:], 1e-5), writes=[cb])
            P.op("pool", lambda E: E.memset(cs["one1"][:], 1.0), writes=[cb])
            P.op("dve", lambda E: E.tensor_copy(cs["ident"][:], cs["ident_f"][:]), reads=[cb], writes=[cb])
            self.hT = self.dscratch("hT", [D, TL], F32)
            self.hbuf = Buf("hT")
            for c in range(DC):
                P.dma("sp", self.hT[c * 128:(c + 1) * 128, :], self.io["xT"][c * 128:(c + 1) * 128, :])
            self.gather_weights()
            self.end_stage()

            for li in range(4):
                if li >= cfg.get("nlayers", 4):
                    break
                kind = li % 3
                if kind == 0:
                    self.retention_layer(li)
                elif kind == 1:
                    self.na_layer(li)
                else:
                    self.mla_layer(li)
                if cfg.get("stop") == "mix%d" % li:
                    break
                self.ffn_layer(li)
                if cfg.get("stop") == "ffn%d" % li:
                    break

            if cfg.get("dump") == "tabs":
                self.end_stage()
            elif cfg.get("dump") == "ret2":
                scr = self.ret_scr
                o = self.io["outT"]
                P.dma("pool", o[0:1024, :], scr["yT"][1536:2560, :])
                P.dma("pool", o[1024:1536, :], scr["yT"][2560:3072, :])
                self.end_stage()
            elif cfg.get("dump") == "ret":
                scr = self.ret_scr
                o = self.io["outT"]
                hd = cfg.get("dumphead", 0)
                P.dma("pool", o[0:256, :], scr["qT"][hd * 256:(hd + 1) * 256, :])
                P.dma("pool", o[256:512, :], scr["kT"][hd * 256:(hd + 1) * 256, :])
                P.dma("pool", o[512:1024, :], scr["yT"][hd * 512:(hd + 1) * 512, :])
                P.dma("pool", o[1024:2048, 0:512], scr["v"][0:1024, hd * 512:(hd + 1) * 512])
                P.dma("pool", o[1024:2048, 512:1024], scr["v"][1024:2048, hd * 512:(hd + 1) * 512])
                P.dma("pool", o[1024:2048, 1024:1536], scr["sg"][0:1024, hd * 512:(hd + 1) * 512])
                P.dma("pool", o[1024:2048, 1536:2048], scr["sg"][1024:2048, hd * 512:(hd + 1) * 512])
                self.end_stage()
            elif cfg.get("stop") is None:
                self.final_norm()
            else:
                for c in range(DC):
                    P.dma("sp", self.io["outT"][c * 128:(c + 1) * 128, :], self.hT[c * 128:(c + 1) * 128, :])
                self.end_stage()
        return nc

    def declare_io(self):
        io = self.io = {}
        io["xT"] = self.din("xT", [D, TL])
        io["outT"] = self.nc.dram_tensor("outT", [D, TL], F32, kind="ExternalOutput").ap()
        io["gains"] = self.din("gains", [128, 9, DC])
        io["ident"] = self.din("ident", [128, 128])
        io["m01"] = self.din("m01", [128, 2])
        io["convp"] = self.din("convp", [128, 4, 2 * FC, 4])
        io["rcos"] = self.din("rcos", [128, TL])
        io["rsin"] = self.din("rsin", [128, TL])
        io["rtabs"] = self.din("rtabs", [128, 4, 128])
        io["qi"] = self.din("qi", [128, 2, 512])
        io["ki"] = self.din("ki", [128, 2])
        io["rdec"] = self.din("rdec", [128, 2, 16])
        if self.cfg.get("nlayers", 4) > 1:
            io["na_bias"] = self.din("na_bias", [16, 25, 128, 128], BF16)
        if self.cfg.get("nlayers", 4) > 2:
            io["mla_gains"] = self.din("mla_gains", [128, 2, 4])
            io["mcos"] = self.din("mcos", [32, TL])
            io["msin"] = self.din("msin", [32, TL])
        self.wshapes = weight_shapes(self.cfg)
        nco = self.cfg.get("ncores", 8)
        self.wsh = {}
        for name, (K_, N_) in self.wshapes.items():
            self.wsh[name] = self.din("w_" + name, [K_ // nco, N_])
            io[name] = self.dscratch("wf_" + name, [K_, N_], F32)

    def gather_weights(self):
        P = self.P
        if self.cfg.get("skipw"):
            return
        nco = self.cfg.get("ncores", 8)
        for name, (K_, N_) in self.wshapes.items():
            rows = K_ // nco
            pr = piece_rows(K_, N_, nco)
            bounce = self.dscratch("wb_" + name, [rows, N_], F32)
            bb = Buf()
            step = max(1, (1 << 21) // (N_ * 4))
            for r0 in range(0, rows, step):
                r1 = min(rows, r0 + step)
                P.dma("sp", bounce[r0:r1, :], self.wsh[name][r0:r1, :], writes=[bb])
            for p in range(rows // pr):
                P.allgather(bounce[p * pr:(p + 1) * pr, :], self.io[name][p * nco * pr:(p + 1) * nco * pr, :],
                            reads=[bb], allranks=True)

    def declare_io_na(self):
        pass

    def declare_io_mla(self):
        pass

    def psum(self):
        i = self.psi
        self.psi = (self.psi + 1) % 6
        return self.ps[i], self.psb[i]

    def psumb(self):
        i = self.pbi
        self.pbi = (self.pbi + 1) % 2
        return self.pb[i], self.pbb[i]

    def norm_stage(self, st, gain_idx, aT, abuf, col0=0):
        P = self.P
        cs = self.consts
        with contextlib.ExitStack() as s2:
            hts = [self.sb(s2, "nh%d" % i, [128, DC, 512], F32) for i in range(2)]
            hbs = [Buf() for _ in range(2)]
            sqs = [self.sb(s2, "nsq%d" % i, [128, 512], F32) for i in range(4)]
            sqb = [Buf() for _ in range(4)]
            rstd = [self.sb(s2, "nrstd%d" % i, [128, 512], F32) for i in range(2)]
            rb = [Buf() for _ in range(2)]
            hview = self.hT.rearrange("(c p) t -> p c t", p=128)
            for ti in range(NT):
                ht, hb = hts[ti % 2], hbs[ti % 2]
                for c8 in range(2):
                    P.dma("sp", ht[:, c8 * 8:(c8 + 1) * 8, :], hview[:, c8 * 8:(c8 + 1) * 8, ti * 512:(ti + 1) * 512], writes=[hb])
                pt, pbuf = self.psum()
                for c in range(DC):
                    sq, sb_ = sqs[c % 4], sqb[c % 4]
                    P.op("act", lambda E, sq=sq, ht=ht, c=c: E.activation(sq[:], ht[:, c, :], AF.Square),
                         reads=[hb], writes=[sb_])
                    P.op("pe", lambda E, pt=pt, sq=sq, c=c: E.matmul(pt[:, :], cs["ones"][:, :], sq[:, :],
                                                                     start=(c == 0), stop=(c == DC - 1)),
                         reads=[sb_, self.cbuf], writes=[pbuf], inc=True)
                rs, rbuf = rstd[ti % 2], rb[ti % 2]
                P.op("act", lambda E, rs=rs, pt=pt: E.activation(rs[:], pt[:, :], AF.Sqrt, scale=1.0 / D,
                                                                 bias=cs["eps6"][:, 0:1]),
                     reads=[pbuf, self.cbuf], writes=[rbuf])
                P.op("dve", lambda E, rs=rs: E.reciprocal(rs[:], rs[:]), reads=[rbuf], writes=[rbuf])
                for c in range(DC):
                    eng = "dve"
                    P.op(eng, lambda E, c=c, ht=ht, rs=rs, ti=ti: E.scalar_tensor_tensor(
                        aT[:, c, col0 + ti * 512: col0 + (ti + 1) * 512], ht[:, c, :],
                        cs["gains"][:, gain_idx, c:c + 1], rs[:], ALU.mult, ALU.mult),
                        reads=[hb, rbuf, self.cbuf], writes=[abuf])
            self.end_stage()

    def load_w(self, wt, wbuf, W, k0, kc, n0, n):
        src = W[k0:k0 + kc * 128, n0:n0 + n].rearrange("(c p) n -> p c n", p=128)
        step = 8 if n > 128 else 16
        for c0 in range(0, kc, step):
            c1 = min(kc, c0 + step)
            self.P.dma("pool", wt[:, c0:c1, 0:n], src[:, c0:c1, :], writes=[wbuf])

    def outproj_stage(self, W, xT_s, xbuf, K):
        P = self.P
        KC = K // 128
        TH = 1024
        NW = 256
        with contextlib.ExitStack() as st:
            xt = self.sb(st, "opx", [128, KC, TH], BF16)
            xb = Buf()
            wts = [self.sb(st, "opw%d" % i, [128, KC, NW], BF16) for i in range(2)]
            wbs = [Buf() for _ in range(2)]
            hts = [self.sb(st, "oph%d" % i, [128, 512], F32) for i in range(3)]
            hbs = [Buf() for _ in range(3)]
            hi = 0
            xview = xT_s.rearrange("(c p) t -> p c t", p=128)
            wi = 0
            for th in range(TL // TH):
                for c0 in range(0, KC, 8):
                    c1 = min(KC, c0 + 8)
                    P.dma("sp", xt[:, c0:c1, :], xview[:, c0:c1, th * TH:(th + 1) * TH], writes=[xb])
                for ng in range(D // NW):
                    wt, wb = wts[wi % 2], wbs[wi % 2]
                    wi += 1
                    self.load_w(wt, wb, W, 0, KC, ng * NW, NW)
                    for nj in range(NW // 128):
                        nchunk = ng * (NW // 128) + nj
                        for tt in range(TH // 512):
                            t0 = th * TH + tt * 512
                            ht, hb = hts[hi % 3], hbs[hi % 3]
                            hi += 1
                            hsl = self.hT[nchunk * 128:(nchunk + 1) * 128, t0:t0 + 512]
                            P.dma("sp", ht[:], hsl, writes=[hb])
                            pt, pbuf = self.psum()
                            for kc in range(KC):
                                P.op("pe", lambda E, pt=pt, wt=wt, xt=xt, kc=kc, nj=nj, tt=tt: E.matmul(
                                    pt[:, :], wt[:, kc, nj * 128:(nj + 1) * 128], xt[:, kc, tt * 512:(tt + 1) * 512],
                                    start=(kc == 0), stop=(kc == KC - 1)),
                                    reads=[wb, xb], writes=[pbuf], inc=(kc == KC - 1))
                            P.op("dve", lambda E, ht=ht, pt=pt: E.tensor_tensor(ht[:], ht[:], pt[:, :], ALU.add),
                                 reads=[pbuf, hb], writes=[hb])
                            P.dma("sp", hsl, ht[:], reads=[hb])
            self.end_stage()

    def retention_layer(self, li):
        P = self.P
        cs = self.consts
        io = self.io
        ri = 0 if li == 0 else 1
        W_in = io["l%d_ret_w_in" % li]
        W_out = io["l%d_ret_w_out" % li]
        if not hasattr(self, "ret_scr"):
            self.ret_scr = dict(
                qT=self.dscratch("r_qT", [D, TL], BF16), kT=self.dscratch("r_kT", [D, TL], BF16),
                v=self.dscratch("r_v", [TL, 4096], BF16), sg=self.dscratch("r_sg", [TL, 4096], BF16),
                yT=self.dscratch("r_yT", [4096, TL], BF16),
                xin=[self.dscratch("r_xin%d" % i, [256, 512], F32) for i in range(2)],
                xout=[self.dscratch("r_xout%d" % i, [512, 512], F32) for i in range(2)],
            )
            self.ret_sb = dict(qT=Buf(), kT=Buf(), v=Buf(), sg=Buf(), yT=Buf(),
                               xin=[Buf(), Buf()], xout=[Buf(), Buf()])
        scr, sbf = self.ret_scr, self.ret_sb
        try:
            self.ret_proj(li, W_in, scr, sbf)
        except _Stop:
            pass
        if self.cfg.get("substop") in ("proj", "norm"):
            return
        self.ret_core_outer(li, ri, W_out, scr, sbf)

    def ret_proj(self, li, W_in, scr, sbf):
        P = self.P
        cs = self.consts
        io = self.io
        with contextlib.ExitStack() as st:
            if self.cfg.get("skipproj"):
                st = None
            aT = self.sb(st, "aT", [128, DC, TL], BF16)
            abuf = Buf("aT")
            self.norm_stage(st, li, aT, abuf)
            if self.cfg.get("substop") == "norm":
                raise _Stop()
            rcos = self.sb(st, "rcos", [128, TL], F32)
            rsin = self.sb(st, "rsin", [128, TL], F32)
            tb = Buf()
            P.dma("sp", rcos[:], io["rcos"], writes=[tb])
            P.dma("sp", rsin[:], io["rsin"], writes=[tb])
            wts = [self.sb(st, "pw%d" % i, [128, DC, 512], BF16) for i in range(3)]
            wbs = [Buf() for _ in range(3)]
            wi = 0
            tmp = [self.sb(st, "ptmp%d" % i, [128, 512], F32) for i in range(4)]
            tmpb = [Buf() for _ in range(4)]
            qst = [self.sb(st, "pqst%d" % i, [128, 2, 512], BF16) for i in range(2)]
            qsb = [Buf() for _ in range(2)]
            qsi = 0
            vst = [self.sb(st, "pvst%d" % i, [128, 4, 512], BF16) for i in range(2)]
            vsb = [Buf() for _ in range(2)]
            vsi = 0
            for which, dst, dbuf in ((0, scr["qT"], sbf["qT"]), (1, scr["kT"], sbf["kT"])):
                for wg in range(4):
                    wt, wb = wts[wi % 3], wbs[wi % 3]
                    wi += 1
                    self.load_w(wt, wb, W_in, 0, DC, which * 2048 + wg * 512, 512)
                    for hh in range(2):
                        h = wg * 2 + hh
                        for ti in range(NT):
                            pp = []
                            for dc in range(2):
                                pt, pbuf = self.psum()
                                col = hh * 256 + dc * 128
                                for kc in range(DC):
                                    P.op("pe", lambda E, pt=pt, wt=wt, kc=kc, col=col, ti=ti: E.matmul(
                                        pt[:, :], wt[:, kc, col:col + 128], aT[:, kc, ti * 512:(ti + 1) * 512],
                                        start=(kc == 0), stop=(kc == DC - 1)),
                                        reads=[wb, abuf], writes=[pbuf], inc=(kc == DC - 1))
                                pp.append((pt, pbuf))
                            (p0, b0), (p1, b1) = pp
                            cosv = rcos[:, ti * 512:(ti + 1) * 512]
                            sinv = rsin[:, ti * 512:(ti + 1) * 512]
                            qs, qb = qst[qsi % 2], qsb[qsi % 2]
                            qsi += 1
                            P.op("dve", lambda E, p0=p0, cosv=cosv: E.tensor_tensor(tmp[0][:], p0[:, :], cosv, ALU.mult),
                                 reads=[b0, tb], writes=[tmpb[0]])
                            P.op("dve", lambda E, p1=p1, sinv=sinv: E.tensor_tensor(tmp[1][:], p1[:, :], sinv, ALU.mult),
                                 reads=[b1, tb], writes=[tmpb[1]])
                            P.op("dve", lambda E, p0=p0, sinv=sinv: E.tensor_tensor(tmp[2][:], p0[:, :], sinv, ALU.mult),
                                 reads=[b0, tb], writes=[tmpb[2]])
                            P.op("dve", lambda E, p1=p1, cosv=cosv: E.tensor_tensor(tmp[3][:], p1[:, :], cosv, ALU.mult),
                                 reads=[b1, tb], writes=[tmpb[3]])
                            P.op("pool", lambda E, qs=qs: E.tensor_tensor(qs[:, 0, :], tmp[0][:], tmp[1][:], ALU.subtract),
                                 reads=[tmpb[0], tmpb[1]], writes=[qb])
                            P.op("pool", lambda E, qs=qs: E.tensor_tensor(qs[:, 1, :], tmp[2][:], tmp[3][:], ALU.add),
                                 reads=[tmpb[2], tmpb[3]], writes=[qb])
                            dsl = dst[h * 256:(h + 1) * 256, ti * 512:(ti + 1) * 512].rearrange("(c p) t -> p c t", p=128)
                            P.dma("sp", dsl, qs[:], reads=[qb])
            for which, dst, dbuf in ((0, scr["v"], sbf["v"]), (1, scr["sg"], sbf["sg"])):
                for h in range(RH):
                    wt, wb = wts[wi % 3], wbs[wi % 3]
                    wi += 1
                    self.load_w(wt, wb, W_in, 0, DC, 4096 + which * 4096 + h * 512, 512)
                    for tg in range(4):
                        vs, vb = vst[vsi % 2], vsb[vsi % 2]
                        vsi += 1
                        for tj in range(4):
                            tc_ = tg * 4 + tj
                            pt, pbuf = self.psum()
                            for kc in range(DC):
                                P.op("pe", lambda E, pt=pt, wt=wt, kc=kc, tc_=tc_: E.matmul(
                                    pt[:, :], aT[:, kc, tc_ * 128:(tc_ + 1) * 128], wt[:, kc, :],
                                    start=(kc == 0), stop=(kc == DC - 1)),
                                    reads=[wb, abuf], writes=[pbuf], inc=(kc == DC - 1))
                            fn = AF.Silu if which == 1 else AF.Copy
                            P.op("act", lambda E, vs=vs, tj=tj, pt=pt, fn=fn: E.activation(vs[:, tj, :], pt[:, :], fn),
                                 reads=[pbuf], writes=[vb])
                        dsl = dst[tg * 512:(tg + 1) * 512, h * 512:(h + 1) * 512].rearrange("(c p) n -> p c n", p=128)
                        P.dma("sp", dsl, vs[:], reads=[vb])
            self.end_stage()

    def ret_core_outer(self, li, ri, W_out, scr, sbf):
        with contextlib.ExitStack() as st:
            try:
                self.ret_core(st, li, ri, scr, sbf)
            except _Stop:
                self.end_stage()
                return
        if self.cfg.get("substop") == "core":
            return
        self.outproj_stage(W_out, scr["yT"], sbf["yT"], 4096)

    def ret_core(self, st, li, ri, scr, sbf):
        P = self.P
        cs = self.consts
        io = self.io
        if True:
            rt = self.sb(st, "rtabs", [128, 4, 128], F32)
            qi = self.sb(st, "qi", [128, 2, 512], F32)
            ki = self.sb(st, "ki", [128, 2], F32)
            dec = self.sb(st, "dec", [128, 16], F32)
            lg = self.sb(st, "lg", [128, 16], F32)
            gC = self.sb(st, "gC", [128, 16], F32)
            kdec = self.sb(st, "kdec", [128, 16], F32)
            dcomb = self.sb(st, "dcomb", [128, RH, 128], F32)
            dtmp = self.sb(st, "dtmp", [128, 2, 128], F32)
            tbuf = Buf()
            P.dma("sp", rt[:], io["rtabs"], writes=[tbuf])
            P.dma("sp", qi[:], io["qi"], writes=[tbuf])
            P.dma("sp", ki[:], io["ki"], writes=[tbuf])
            P.dma("sp", dec[:], io["rdec"][:, ri, :], writes=[tbuf])
            P.op("act", lambda E: E.activation(lg[:], dec[:], AF.Exp, scale=-1.0), reads=[tbuf], writes=[tbuf])
            P.op("act", lambda E: E.activation(lg[:], lg[:], AF.Ln, bias=cs["one1"][:, 0:1]), reads=[tbuf, self.cbuf], writes=[tbuf])
            P.op("dve", lambda E: E.tensor_scalar(lg[:], lg[:], -1.0, None, ALU.mult), reads=[tbuf], writes=[tbuf])
            P.op("act", lambda E: E.activation(gC[:], lg[:], AF.Exp, scale=128.0), reads=[tbuf], writes=[tbuf])
            lnk = math.log(1.0 / 16.0)
            lnkb = self.sb(st, "lnkb", [128, 1], F32)
            P.op("pool", lambda E: E.memset(lnkb[:], lnk), writes=[tbuf])
            for h in range(RH):
                for d_ in range(2):
                    col = d_ * 8 + h
                    P.op("act", lambda E, col=col, d_=d_: E.activation(kdec[:, col:col + 1], ki[:, d_:d_ + 1], AF.Exp,
                                                                       scale=lg[:, col:col + 1], bias=lnkb[:, 0:1]),
                         reads=[tbuf], writes=[tbuf])
                    P.op("act", lambda E, col=col, d_=d_: E.activation(dtmp[:, d_, :], rt[:, d_, :], AF.Exp,
                                                                       scale=lg[:, col:col + 1], bias=lnkb[:, 0:1]),
                         reads=[tbuf], writes=[tbuf])
                    P.op("dve", lambda E, d_=d_: E.tensor_tensor(dtmp[:, d_, :], dtmp[:, d_, :], rt[:, 2 + d_, :], ALU.mult),
                         reads=[tbuf], writes=[tbuf])
                P.op("dve", lambda E, h=h: E.tensor_tensor(dcomb[:, h, :], dtmp[:, 0, :], dtmp[:, 1, :], ALU.add),
                     reads=[tbuf], writes=[tbuf])

            qT = self.sb(st, "cqT", [128, 2, TL], BF16)
            kT = self.sb(st, "ckT", [128, 2, TL], BF16)
            qd = [self.sb(st, "cqd%d" % i, [128, 2, TL], BF16) for i in range(2)]
            kd = [self.sb(st, "ckd%d" % i, [128, 16, 256], BF16) for i in range(2)]
            v = self.sb(st, "cv", [128, 16, 512], BF16)
            sg = self.sb(st, "csg", [128, 16, 512], BF16)
            oacc = self.sb(st, "coacc", [128, 16, 512], F32)
            yT = self.sb(st, "cyT", [128, 4, TL], BF16)
            tab = self.sb(st, "ctab", [128, 2, 512], F32)
            stf = [self.sb(st, "cstf%d" % i, [128, 2, 512], F32) for i in range(2)]
            stb = [self.sb(st, "cstb%d" % i, [128, 2, 512], BF16) for i in range(2)]
            xld = self.sb(st, "cxld", [128, 4, 512], F32)
            stm = [self.sb(st, "cstm%d" % i, [128, 128], BF16) for i in range(2)]
            osb = [self.sb(st, "cosb%d" % i, [128, 512], F32) for i in range(2)]
            ysb = [self.sb(st, "cysb%d" % i, [128, 512], BF16) for i in range(2)]
            sgr = [self.sb(st, "csgr%d" % i, [128, 512], F32) for i in range(2)]
            bn6 = [self.sb(st, "cbn6%d" % i, [128, 6], F32) for i in range(2)]
            mv = [self.sb(st, "cmv%d" % i, [128, 2], F32) for i in range(2)]
            b_q, b_k, b_v, b_sg, b_oacc, b_yT, b_tab = Buf(), Buf(), Buf(), Buf(), Buf(), Buf(), Buf()
            b_qd = [Buf(), Buf()]
            b_kd = [Buf(), Buf()]
            b_stf = [Buf(), Buf()]
            b_stb = [Buf(), Buf()]
            b_xld = Buf()
            b_stm = [Buf(), Buf()]
            b_osb = [Buf(), Buf()]
            b_ysb = [Buf(), Buf()]
            b_sgr = [Buf(), Buf()]
            b_bn = [Buf(), Buf()]
            it = 0
            if self.cfg.get("dump") == "tabs":
                o = self.io["outT"]
                P.dma("sp", o[0:128, 0:16], lg[:], reads=[tbuf])
                P.dma("sp", o[0:128, 16:32], gC[:], reads=[tbuf])
                P.dma("sp", o[0:128, 32:48], kdec[:], reads=[tbuf])
                P.dma("sp", o[128:256, 0:1024], dcomb[:].rearrange("p a b -> p (a b)"), reads=[tbuf])
            self.cstop(0)
            for h in range(RH):
                P.dma("sp", qT[:], scr["qT"][h * 256:(h + 1) * 256, :].rearrange("(c p) t -> p c t", p=128),
                      writes=[b_q])
                P.dma("sp", kT[:], scr["kT"][h * 256:(h + 1) * 256, :].rearrange("(c p) t -> p c t", p=128),
                      writes=[b_k])
                P.dma("sp", v[:], scr["v"][:, h * 512:(h + 1) * 512].rearrange("(c p) n -> p c n", p=128),
                      writes=[b_v])
                P.dma("sp", sg[:], scr["sg"][:, h * 512:(h + 1) * 512].rearrange("(c p) n -> p c n", p=128),
                      writes=[b_sg])
                for d_ in range(2):
                    col = d_ * 8 + h
                    P.op("act", lambda E, d_=d_, col=col: E.activation(tab[:, d_, :], qi[:, d_, :], AF.Exp,
                                                                       scale=lg[:, col:col + 1]),
                         reads=[tbuf], writes=[b_tab])
                for d_ in range(2):
                    for dc in range(2):
                        for ti in range(NT):
                            eng = "pool" if (dc + ti) % 2 == 0 else "dve"
                            if self.cfg.get("nopool"):
                                eng = "dve"
                            P.op(eng, lambda E, d_=d_, dc=dc, ti=ti: E.tensor_tensor(
                                qd[d_][:, dc, ti * 512:(ti + 1) * 512], qT[:, dc, ti * 512:(ti + 1) * 512],
                                tab[:, d_, :], ALU.mult),
                                reads=[b_q, b_tab], writes=[b_qd[d_]])
                self.cstop(1)
                for g4 in range(4):
                    pb, pbb = self.psumb()
                    for nl in range(4):
                        n = g4 * 4 + nl
                        for dc in range(2):
                            o0 = (nl * 2 + dc) * 128
                            P.op("pe", lambda E, pb=pb, o0=o0, dc=dc, n=n: E.transpose(
                                pb[:, o0:o0 + 128], kT[:, dc, n * 128:(n + 1) * 128], cs["ident"][:]),
                                reads=[b_k, self.cbuf], writes=[pbb])
                    for d_ in range(2):
                        col = d_ * 8 + h
                        eng = "act"
                        if self.cfg.get("kt_pe") or (self.cfg.get("kt_act") and eng != "act") or (self.cfg.get("kt_dve") and eng != "dve"):
                            continue
                        if eng == "act":
                            P.op("act", lambda E, pb=pb, d_=d_, col=col, g4=g4: E.activation(
                                kd[d_][:, g4 * 4:(g4 + 1) * 4, :].rearrange("p a b -> p (a b)"), pb[:, :], AF.Identity,
                                scale=kdec[:, col:col + 1]),
                                reads=[pbb, tbuf], writes=[b_kd[d_]])
                        else:
                            P.op("dve", lambda E, pb=pb, d_=d_, col=col, g4=g4: E.tensor_scalar(
                                kd[d_][:, g4 * 4:(g4 + 1) * 4, :].rearrange("p a b -> p (a b)"), pb[:, :],
                                kdec[:, col:col + 1], None, ALU.mult),
                                reads=[pbb, tbuf], writes=[b_kd[d_]])

                def state_update(d_, n, first):
                    col = d_ * 8 + h
                    for dc in range(2):
                        pt, pbuf = self.psum()
                        P.op("pe", lambda E, pt=pt, d_=d_, n=n, dc=dc: E.matmul(
                            pt[:, :], kd[d_][:, n, dc * 128:(dc + 1) * 128], v[:, n, :], start=True, stop=True),
                            reads=[b_kd[d_], b_v], writes=[pbuf])
                        if first:
                            P.op("dve", lambda E, pt=pt, d_=d_, dc=dc: E.tensor_copy(stf[d_][:, dc, :], pt[:, :]),
                                 reads=[pbuf], writes=[b_stf[d_]])
                        else:
                            P.op("dve", lambda E, pt=pt, d_=d_, dc=dc, col=col: E.scalar_tensor_tensor(
                                stf[d_][:, dc, :], stf[d_][:, dc, :], gC[:, col:col + 1], pt[:, :], ALU.mult, ALU.add),
                                reads=[pbuf, b_stf[d_], tbuf], writes=[b_stf[d_]])
                    P.op("act", lambda E, d_=d_: E.copy(stb[d_][:].rearrange("p a b -> p (a b)"),
                                                        stf[d_][:].rearrange("p a b -> p (a b)")),
                         reads=[b_stf[d_]], writes=[b_stb[d_]])

                self.cstop(2)
                for n in range(16):
                    pS, pSb = self.psum()
                    for dc in range(2):
                        P.op("pe", lambda E, pS=pS, dc=dc, n=n: E.matmul(
                            pS[:, 0:128], kT[:, dc, n * 128:(n + 1) * 128], qT[:, dc, n * 128:(n + 1) * 128],
                            start=(dc == 0), stop=(dc == 1)),
                            reads=[b_k, b_q], writes=[pSb], inc=(dc == 1))
                    sm, smb = stm[n % 2], b_stm[n % 2]
                    P.op("dve", lambda E, sm=sm, pS=pS, h=h: E.tensor_tensor(sm[:], pS[:, 0:128], dcomb[:, h, :], ALU.mult),
                         reads=[pSb, tbuf], writes=[smb])
                    pO, pOb = self.psum()
                    nmm = 1 if n == 0 else 3
                    P.op("pe", lambda E, pO=pO, sm=sm, n=n, nmm=nmm: E.matmul(
                        pO[:, :], sm[:], v[:, n, :], start=True, stop=(nmm == 1)),
                        reads=[smb, b_v], writes=[pOb], inc=(nmm == 1))
                    if n > 0:
                        for dc in range(2):
                            P.op("pe", lambda E, pO=pO, n=n, dc=dc: E.matmul(
                                pO[:, :], qd[0][:, dc, n * 128:(n + 1) * 128], stb[0][:, dc, :],
                                start=False, stop=(dc == 1)),
                                reads=[b_qd[0], b_stb[0]], writes=[pOb], inc=(dc == 1))
                    P.op("act", lambda E, pO=pO, n=n: E.copy(oacc[:, n, :], pO[:, :]), reads=[pOb], writes=[b_oacc])
                    state_update(0, n, n == 0)
                self.cstop(3)
                xi = h % 2
                xin, xout = scr["xin"][xi], scr["xout"][xi]
                P.dma("sp", xin.rearrange("(c p) n -> p c n", p=128), stf[0][:], reads=[b_stf[0]],
                      writes=[sbf["xin"][xi]])
                P.allgather(xin, xout, reads=[sbf["xin"][xi]], writes=[sbf["xout"][xi]])
                P.dma("sp", xld[:], xout.rearrange("(c p) n -> p c n", p=128), reads=[sbf["xout"][xi]], writes=[b_xld])
                for dc in range(2):
                    P.op("dve", lambda E, dc=dc: E.tensor_scalar(stf[1][:, dc, :], xld[:, dc, :], cs["m01"][:, 0:1], None,
                                                                 ALU.mult),
                         reads=[b_xld, self.cbuf], writes=[b_stf[1]])
                    P.op("dve", lambda E, dc=dc: E.scalar_tensor_tensor(stf[1][:, dc, :], xld[:, 2 + dc, :],
                                                                        cs["m01"][:, 1:2], stf[1][:, dc, :],
                                                                        ALU.mult, ALU.add),
                         reads=[b_xld, self.cbuf, b_stf[1]], writes=[b_stf[1]])
                P.op("act", lambda E: E.copy(stb[1][:].rearrange("p a b -> p (a b)"),
                                             stf[1][:].rearrange("p a b -> p (a b)")),
                     reads=[b_stf[1]], writes=[b_stb[1]])
                self.cstop(4)
                for n in range(15, -1, -1):
                    pO, pOb = self.psum()
                    for dc in range(2):
                        P.op("pe", lambda E, pO=pO, n=n, dc=dc: E.matmul(
                            pO[:, :], qd[1][:, dc, n * 128:(n + 1) * 128], stb[1][:, dc, :],
                            start=(dc == 0), stop=(dc == 1)),
                            reads=[b_qd[1], b_stb[1]], writes=[pOb], inc=(dc == 1))
                    k2 = it % 2
                    it += 1
                    o_, ob_ = osb[k2], b_osb[k2]
                    P.op("dve", lambda E, o_=o_, pO=pO, n=n: E.tensor_tensor(o_[:], pO[:, :], oacc[:, n, :], ALU.add),
                         reads=[pOb, b_oacc], writes=[ob_])
                    P.op("dve", lambda E, o_=o_, k2=k2: E.bn_stats(bn6[k2][:], o_[:]), reads=[ob_], writes=[b_bn[k2]])
                    P.op("dve", lambda E, k2=k2: E.bn_aggr(mv[k2][:], bn6[k2][:]), reads=[b_bn[k2]], writes=[b_bn[k2]])
                    P.op("act", lambda E, k2=k2: E.activation(mv[k2][:, 1:2], mv[k2][:, 1:2], AF.Sqrt,
                                                              bias=cs["eps5"][:, 0:1]),
                         reads=[b_bn[k2], self.cbuf], writes=[b_bn[k2]])
                    P.op("dve", lambda E, k2=k2: E.reciprocal(mv[k2][:, 1:2], mv[k2][:, 1:2]),
                         reads=[b_bn[k2]], writes=[b_bn[k2]])
                    P.op("act", lambda E, k2=k2, n=n: E.activation(sgr[k2][:], sg[:, n, :], AF.Identity,
                                                                   scale=mv[k2][:, 1:2]),
                         reads=[b_sg, b_bn[k2]], writes=[b_sgr[k2]])
                    P.op("dve", lambda E, k2=k2, o_=o_: E.scalar_tensor_tensor(
                        ysb[k2][:], o_[:], mv[k2][:, 0:1], sgr[k2][:], ALU.subtract, ALU.mult),
                        reads=[ob_, b_bn[k2], b_sgr[k2]], writes=[b_ysb[k2]])
                    pb, pbb = self.psumb()
                    for c4 in range(4):
                        P.op("pe", lambda E, pb=pb, c4=c4, k2=k2: E.transpose(
                            pb[:, c4 * 128:(c4 + 1) * 128], ysb[k2][:, c4 * 128:(c4 + 1) * 128], cs["ident"][:]),
                            reads=[b_ysb[k2], self.cbuf], writes=[pbb])
                    P.op("act", lambda E, pb=pb, n=n: E.copy(
                        yT[:, :, n * 128:(n + 1) * 128], pb[:, 0:512].rearrange("p (a b) -> p a b", a=4)),
                        reads=[pbb], writes=[b_yT])
                    if n > 0:
                        state_update(1, n, False)
                P.dma("sp", scr["yT"][h * 512:(h + 1) * 512, :].rearrange("(c p) t -> p c t", p=128), yT[:],
                      reads=[b_yT])
                self.cstop(5)
            self.end_stage()

    def ffn_layer(self, li):
        P = self.P
        cs = self.consts
        io = self.io
        W_up = io["l%d_ffn_w_up" % li]
        W_dn = io["l%d_ffn_w_down" % li]
        if not hasattr(self, "ffn_scr"):
            self.ffn_scr = dict(act=self.dscratch("f_act", [DFF, TL], BF16),
                                xin=self.dscratch("f_xin", [128, DC], F32),
                                xout=self.dscratch("f_xout", [256, DC], F32))
            self.ffn_sb = dict(act=Buf(), xin=Buf(), xout=Buf())
        scr, sbf = self.ffn_scr, self.ffn_sb
        with contextlib.ExitStack() as st:
            aT = self.sb(st, "faT", [128, DC, TL], BF16)
            abuf = Buf("faT")
            self.norm_stage(st, 4 + li, aT, abuf)
            hl = self.sb(st, "fhl", [128, DC], F32)
            hx = self.sb(st, "fhx", [128, 2, DC], F32)
            hb16 = self.sb(st, "fhb", [128, DC], BF16)
            hlb = Buf()
            P.op("dve", lambda E: E.tensor_copy(hl[:], aT[:, :, TL - 1]), reads=[abuf], writes=[hlb])
            P.dma("sp", scr["xin"], hl[:], reads=[hlb], writes=[sbf["xin"]])
            P.allgather(scr["xin"], scr["xout"], reads=[sbf["xin"]], writes=[sbf["xout"]])
            P.dma("sp", hx[:], scr["xout"].rearrange("(r p) c -> p r c", p=128), reads=[sbf["xout"]], writes=[hlb])
            P.op("dve", lambda E: E.tensor_scalar(hl[:], hx[:, 0, :], cs["m01"][:, 0:1], None, ALU.mult),
                 reads=[hlb, self.cbuf], writes=[hlb])
            P.op("dve", lambda E: E.scalar_tensor_tensor(hb16[:], hx[:, 1, :], cs["m01"][:, 1:2], hl[:], ALU.mult, ALU.add),
                 reads=[hlb, self.cbuf], writes=[hlb])
            cp = self.sb(st, "fcp", [128, 2 * FC, 4], F32)
            cpb = Buf()
            P.dma("sp", cp[:], io["convp"][:, li, :, :], writes=[cpb])
            wts = [self.sb(st, "fw%d" % i, [128, DC, 128], BF16) for i in range(4)]
            wbs = [Buf() for _ in range(4)]
            wi = 0
            ub = [self.sb(st, "fu%d" % i, [128, TL + 2], F32) for i in range(2)]
            ubb = [Buf() for _ in range(2)]
            cv = [self.sb(st, "fc%d" % i, [128, TL], F32) for i in range(2)]
            cvb = [Buf() for _ in range(2)]
            ast = [self.sb(st, "fa%d" % i, [128, TL], BF16) for i in range(2)]
            asb = [Buf() for _ in range(2)]
            for i in range(2):
                P.op("pool", lambda E, i=i: E.memset(ub[i][:, 0:1], 0.0), writes=[ubb[i]])
            for j in range(FC):
                for gv in range(2):
                    fch = gv * FC + j
                    wt, wb = wts[wi % 4], wbs[wi % 4]
                    wi += 1
                    self.load_w(wt, wb, W_up, 0, DC, fch * 128, 128)
                    u, ubuf = ub[gv], ubb[gv]
                    for ti in range(NT):
                        pt, pbuf = self.psum()
                        for kc in range(DC):
                            P.op("pe", lambda E, pt=pt, wt=wt, kc=kc, ti=ti: E.matmul(
                                pt[:, :], wt[:, kc, :], aT[:, kc, ti * 512:(ti + 1) * 512],
                                start=(kc == 0), stop=(kc == DC - 1)),
                                reads=[wb, abuf], writes=[pbuf], inc=(kc == DC - 1))
                        P.op("act", lambda E, u=u, pt=pt, ti=ti: E.copy(u[:, 1 + ti * 512:1 + (ti + 1) * 512], pt[:, :]),
                             reads=[pbuf], writes=[ubuf])
                    pt, pbuf = self.psum()
                    for kc in range(DC):
                        P.op("pe", lambda E, pt=pt, wt=wt, kc=kc: E.matmul(
                            pt[:, 0:1], wt[:, kc, :], hb16[:, kc:kc + 1], start=(kc == 0), stop=(kc == DC - 1)),
                            reads=[wb, hlb], writes=[pbuf], inc=(kc == DC - 1))
                    P.op("act", lambda E, u=u, pt=pt: E.copy(u[:, TL + 1:TL + 2], pt[:, 0:1]), reads=[pbuf], writes=[ubuf])
                    c_, cb_ = cv[gv], cvb[gv]
                    P.op("act", lambda E, c_=c_, u=u, fch=fch: E.activation(
                        c_[:], u[:, 1:TL + 1], AF.Identity, scale=cp[:, fch, 1:2], bias=cp[:, fch, 3:4]),
                        reads=[ubuf, cpb], writes=[cb_])
                    P.op("dve", lambda E, c_=c_, u=u, fch=fch: E.scalar_tensor_tensor(
                        c_[:], u[:, 0:TL], cp[:, fch, 0:1], c_[:], ALU.mult, ALU.add),
                        reads=[ubuf, cpb, cb_], writes=[cb_])
                    P.op("dve", lambda E, c_=c_, u=u, fch=fch: E.scalar_tensor_tensor(
                        c_[:], u[:, 2:TL + 2], cp[:, fch, 2:3], c_[:], ALU.mult, ALU.add),
                        reads=[ubuf, cpb, cb_], writes=[cb_])
                P.op("act", lambda E: E.activation(cv[0][:], cv[0][:], AF.Silu), reads=[cvb[0]], writes=[cvb[0]])
                a_, ab_ = ast[j % 2], asb[j % 2]
                P.op("pool", lambda E, a_=a_: E.tensor_tensor(a_[:], cv[0][:], cv[1][:], ALU.mult),
                     reads=[cvb[0], cvb[1]], writes=[ab_])
                P.dma("sp", scr["act"][j * 128:(j + 1) * 128, :], a_[:], reads=[ab_])
            self.end_stage()
        self.outproj_stage(W_dn, scr["act"], sbf["act"], DFF)


    def na_layer(self, li):
        P = self.P
        cs = self.consts
        io = self.io
        W = io["l%d_na_w_qkv" % li]
        W_out = io["l%d_na_w_out" % li]
        TE = TL + 256
        scr = dict(qT=self.dscratch("n_qT", [D, TL], BF16), kT=self.dscratch("n_kT", [D, TE], BF16),
                   v=self.dscratch("n_v", [TE, D], BF16), oT=self.dscratch("n_oT", [D, TL], BF16),
                   xin=self.dscratch("n_xin", [D, 256], BF16), xout=self.dscratch("n_xout", [2 * D, 256], BF16))
        with contextlib.ExitStack() as st:
            aT = self.sb(st, "naT", [128, DC, TE], BF16)
            abuf = Buf("naT")
            self.norm_stage(st, li, aT, abuf)
            hx = self.sb(st, "nhx", [128, 2, DC, 256], BF16)
            htmp = self.sb(st, "nhtmp", [128, DC, 256], BF16)
            xb, xo, hb = Buf(), Buf(), Buf()
            xin_v = scr["xin"].rearrange("(c p) t -> p c t", p=128)
            for a_ in range(4):
                P.dma("sp", xin_v[:, :, a_ * 64:(a_ + 1) * 64], aT[:, :, (31 - a_) * 64:(32 - a_) * 64],
                      reads=[abuf], writes=[xb])
            P.allgather(scr["xin"], scr["xout"], reads=[xb], writes=[xo])
            for r_ in range(2):
                P.dma("sp", hx[:, r_, :, :], scr["xout"][r_ * D:(r_ + 1) * D, :].rearrange("(c p) t -> p c t", p=128),
                      reads=[xo], writes=[hb])
            P.op("dve", lambda E: E.tensor_scalar(htmp[:], hx[:, 0, :, :], cs["m01"][:, 0:1], None, ALU.mult),
                 reads=[hb, self.cbuf], writes=[hb])
            P.op("dve", lambda E: E.scalar_tensor_tensor(aT[:, :, TL:TE], hx[:, 1, :, :], cs["m01"][:, 1:2], htmp[:],
                                                         ALU.mult, ALU.add),
                 reads=[hb, self.cbuf], writes=[abuf])
            wts = [self.sb(st, "nw%d" % i, [128, DC, 512], BF16) for i in range(3)]
            wbs = [Buf() for _ in range(3)]
            wi = 0
            qst = [self.sb(st, "nqst%d" % i, [128, TE], BF16) for i in range(2)]
            qsb = [Buf() for _ in range(2)]
            qsi = 0
            vst = [self.sb(st, "nvst%d" % i, [128, 2, 512], BF16) for i in range(2)]
            vsb = [Buf() for _ in range(2)]
            vsi = 0
            qscale = 128.0 ** -0.5
            for which in range(2):
                ncols = TL if which == 0 else TE
                dst = scr["qT"] if which == 0 else scr["kT"]
                for wg in range(4):
                    wt, wb = wts[wi % 3], wbs[wi % 3]
                    wi += 1
                    self.load_w(wt, wb, W, 0, DC, which * 2048 + wg * 512, 512)
                    for hh in range(4):
                        h = wg * 4 + hh
                        qs, qb = qst[qsi % 2], qsb[qsi % 2]
                        qsi += 1
                        t0 = 0
                        while t0 < ncols:
                            tw = min(512, ncols - t0)
                            pt, pbuf = self.psum()
                            for kc in range(DC):
                                P.op("pe", lambda E, pt=pt, wt=wt, kc=kc, hh=hh, t0=t0, tw=tw: E.matmul(
                                    pt[:, 0:tw], wt[:, kc, hh * 128:(hh + 1) * 128], aT[:, kc, t0:t0 + tw],
                                    start=(kc == 0), stop=(kc == DC - 1)),
                                    reads=[wb, abuf], writes=[pbuf], inc=(kc == DC - 1))
                            sc_ = qscale if which == 0 else 1.0
                            P.op("act", lambda E, qs=qs, pt=pt, t0=t0, tw=tw, sc_=sc_: E.activation(
                                qs[:, t0:t0 + tw], pt[:, 0:tw], AF.Copy, scale=sc_),
                                reads=[pbuf], writes=[qb])
                            t0 += tw
                        P.dma("sp", dst[h * 128:(h + 1) * 128, :], qs[:, 0:ncols], reads=[qb])
            for wg in range(4):
                wt, wb = wts[wi % 3], wbs[wi % 3]
                wi += 1
                self.load_w(wt, wb, W, 0, DC, 4096 + wg * 512, 512)
                for tg in range(TE // 256):
                    vs, vb = vst[vsi % 2], vsb[vsi % 2]
                    vsi += 1
                    for tj in range(2):
                        tc_ = tg * 2 + tj
                        pt, pbuf = self.psum()
                        for kc in range(DC):
                            P.op("pe", lambda E, pt=pt, wt=wt, kc=kc, tc_=tc_: E.matmul(
                                pt[:, :], aT[:, kc, tc_ * 128:(tc_ + 1) * 128], wt[:, kc, :],
                                start=(kc == 0), stop=(kc == DC - 1)),
                                reads=[wb, abuf], writes=[pbuf], inc=(kc == DC - 1))
                        P.op("act", lambda E, vs=vs, tj=tj, pt=pt: E.activation(vs[:, tj, :], pt[:, :], AF.Copy),
                             reads=[pbuf], writes=[vb])
                    dsl = scr["v"][tg * 256:(tg + 1) * 256, wg * 512:(wg + 1) * 512].rearrange("(c p) n -> p c n", p=128)
                    P.dma("sp", dsl, vs[:], reads=[vb])
            self.end_stage()

        with contextlib.ExitStack() as st:
            NCH = TE // 128
            qTs = [self.sb(st, "cnq%d" % i, [128, TL], BF16) for i in range(2)]
            kTs = [self.sb(st, "cnk%d" % i, [128, TE], BF16) for i in range(2)]
            vss = [self.sb(st, "cnv%d" % i, [128, NCH, 128], BF16) for i in range(2)]
            bts = [self.sb(st, "cnb%d" % i, [128, 25, 128], BF16) for i in range(2)]
            lbs = [Buf() for _ in range(2)]
            pT = [self.sb(st, "cnp%d" % i, [128, 640], BF16) for i in range(2)]
            pTb = [Buf() for _ in range(2)]
            rec = [self.sb(st, "cnr%d" % i, [128, 128], F32) for i in range(2)]
            recb = [Buf() for _ in range(2)]
            oTs = [self.sb(st, "cno%d" % i, [128, TL], BF16) for i in range(2)]
            oTb = [Buf() for _ in range(2)]
            it = 0
            for h in range(16):
                qT, kT, vv, bt, lb = qTs[h % 2], kTs[h % 2], vss[h % 2], bts[h % 2], lbs[h % 2]
                P.dma("sp", qT[:], scr["qT"][h * 128:(h + 1) * 128, :], writes=[lb])
                P.dma("sp", kT[:], scr["kT"][h * 128:(h + 1) * 128, :], writes=[lb])
                P.dma("sp", vv[:], scr["v"][:, h * 128:(h + 1) * 128].rearrange("(c p) n -> p c n", p=128), writes=[lb])
                P.dma("sp", bt[:], io["na_bias"][h].rearrange("a p q -> p a q"), writes=[lb])
                oT, ob = oTs[h % 2], oTb[h % 2]
                for p in range(16):
                    pat = 0 if p == 0 else 1 if p == 1 else 3 if p == 14 else 4 if p == 15 else 2
                    start = max(2 * p - 4, 0)
                    pA, pAb = self.psum()
                    pB, pBb = self.psum()
                    for c in range(5):
                        k0 = 64 * (start + 2 * c)
                        if c < 4:
                            dst_, dbuf_ = pA[:, c * 128:(c + 1) * 128], pAb
                        else:
                            dst_, dbuf_ = pB[:, 0:128], pBb
                        P.op("pe", lambda E, dst_=dst_, kT=kT, qT=qT, k0=k0, p=p: E.matmul(
                            dst_, kT[:, k0:k0 + 128], qT[:, p * 128:(p + 1) * 128], start=True, stop=False),
                            reads=[lb], writes=[dbuf_], inc=False)
                        P.op("pe", lambda E, dst_=dst_, bt=bt, pat=pat, c=c: E.matmul(
                            dst_, cs["ident"][:], bt[:, pat * 5 + c, :], start=False, stop=True),
                            reads=[lb, self.cbuf], writes=[dbuf_], inc=True)
                    k2 = it % 2
                    it += 1
                    pt_, ptb_ = pT[k2], pTb[k2]
                    P.op("act", lambda E, pt_=pt_, pA=pA: E.activation(pt_[:, 0:512], pA[:, :], AF.Exp),
                         reads=[pAb], writes=[ptb_])
                    P.op("act", lambda E, pt_=pt_, pB=pB: E.activation(pt_[:, 512:640], pB[:, 0:128], AF.Exp),
                         reads=[pBb], writes=[ptb_])
                    pO, pOb = self.psum()
                    for c in range(5):
                        kc_ = (start + 2 * c) // 2
                        P.op("pe", lambda E, pO=pO, vv=vv, kc_=kc_, pt_=pt_, c=c: E.matmul(
                            pO[:, 0:128], vv[:, kc_, :], pt_[:, c * 128:(c + 1) * 128], start=(c == 0), stop=(c == 4)),
                            reads=[lb, ptb_], writes=[pOb], inc=(c == 4))
                    for c in range(5):
                        P.op("pe", lambda E, pO=pO, pt_=pt_, c=c: E.matmul(
                            pO[:, 128:256], cs["onesb"][:], pt_[:, c * 128:(c + 1) * 128], start=(c == 0), stop=(c == 4)),
                            reads=[self.cbuf, ptb_], writes=[pOb], inc=(c == 4))
                    rc, rcb = rec[k2], recb[k2]
                    P.op("dve", lambda E, rc=rc, pO=pO: E.reciprocal(rc[:], pO[:, 128:256]), reads=[pOb], writes=[rcb])
                    P.op("dve", lambda E, oT=oT, pO=pO, rc=rc, p=p: E.tensor_tensor(
                        oT[:, p * 128:(p + 1) * 128], pO[:, 0:128], rc[:], ALU.mult),
                        reads=[pOb, rcb], writes=[ob])
                P.dma("sp", scr["oT"][h * 128:(h + 1) * 128, :], oT[:], reads=[ob])
            self.end_stage()
        self.outproj_stage(W_out, scr["oT"], None, D)

    def mla_layer(self, li):
        P = self.P
        cs = self.consts
        io = self.io
        W_dn = io["l%d_mla_w_down" % li]
        W_uq = io["l%d_mla_w_uq" % li]
        W_ukv = io["l%d_mla_w_ukv" % li]
        W_out = io["l%d_mla_w_out" % li]
        scr = dict(cq=self.dscratch("m_cq", [512, TL], BF16),
                   lat=[self.dscratch("m_lat%d" % i, [576, 1024], BF16) for i in range(2)],
                   lout=[self.dscratch("m_lout%d" % i, [1152, 1024], BF16) for i in range(2)],
                   oT=self.dscratch("m_oT", [D, TL], BF16))
        with contextlib.ExitStack() as st:
            aT = self.sb(st, "maT", [128, DC, TL], BF16)
            abuf = Buf("maT")
            self.norm_stage(st, li, aT, abuf)
            wt = self.sb(st, "mwd", [128, DC, 1088], BF16)
            wb = Buf()
            self.load_w(wt, wb, W_dn, 0, DC, 0, 1088)
            mg = self.sb(st, "mg", [128, 2, 4], F32)
            mcos = self.sb(st, "mcos", [32, TL], F32)
            msin = self.sb(st, "msin", [32, TL], F32)
            tb = Buf()
            P.dma("sp", mg[:], io["mla_gains"], writes=[tb])
            P.dma("sp", mcos[:], io["mcos"], writes=[tb])
            P.dma("sp", msin[:], io["msin"], writes=[tb])
            cf = [self.sb(st, "mcf%d" % i, [128, 4, 512], F32) for i in range(2)]
            cfb = [Buf() for _ in range(2)]
            sq = [self.sb(st, "msq%d" % i, [128, 512], F32) for i in range(2)]
            sqb = [Buf() for _ in range(2)]
            rs = [self.sb(st, "mrs%d" % i, [128, 512], F32) for i in range(2)]
            rsb = [Buf() for _ in range(2)]
            cn = [self.sb(st, "mcn%d" % i, [128, 4, 512], BF16) for i in range(2)]
            cnb = [Buf() for _ in range(2)]
            tmp = [self.sb(st, "mtmp%d" % i, [32, 512], F32) for i in range(4)]
            tmpb = [Buf() for _ in range(4)]
            kr = [self.sb(st, "mkr%d" % i, [32, 2, 512], BF16) for i in range(2)]
            krb = [Buf() for _ in range(2)]
            it = 0
            for ti in range(NT):
                half, tcol = ti // 2, (ti % 2) * 512
                for part in range(2):
                    k2 = it % 2
                    it += 1
                    pS, pSb = self.psum()
                    for c in range(4):
                        pt, pbuf = self.psum()
                        col = part * 512 + c * 128
                        for kc in range(DC):
                            P.op("pe", lambda E, pt=pt, kc=kc, col=col, ti=ti: E.matmul(
                                pt[:, :], wt[:, kc, col:col + 128], aT[:, kc, ti * 512:(ti + 1) * 512],
                                start=(kc == 0), stop=(kc == DC - 1)),
                                reads=[wb, abuf], writes=[pbuf], inc=(kc == DC - 1))
                        P.op("act", lambda E, k2=k2, c=c, pt=pt: E.copy(cf[k2][:, c, :], pt[:, :]),
                             reads=[pbuf], writes=[cfb[k2]])
                        s2 = c % 2
                        P.op("act", lambda E, s2=s2, pt=pt: E.activation(sq[s2][:], pt[:, :], AF.Square),
                             reads=[pbuf], writes=[sqb[s2]])
                        P.op("pe", lambda E, pS=pS, s2=s2, c=c: E.matmul(pS[:, :], cs["ones"][:, :], sq[s2][:, :],
                                                                        start=(c == 0), stop=(c == 3)),
                             reads=[sqb[s2], self.cbuf], writes=[pSb], inc=True)
                    P.op("act", lambda E, k2=k2, pS=pS: E.activation(rs[k2][:], pS[:, :], AF.Sqrt, scale=1.0 / 512,
                                                                     bias=cs["eps6"][:, 0:1]),
                         reads=[pSb, self.cbuf], writes=[rsb[k2]])
                    P.op("dve", lambda E, k2=k2: E.reciprocal(rs[k2][:], rs[k2][:]), reads=[rsb[k2]], writes=[rsb[k2]])
                    for c in range(4):
                        P.op("dve", lambda E, k2=k2, c=c, part=part: E.scalar_tensor_tensor(
                            cn[k2][:, c, :], cf[k2][:, c, :], mg[:, part, c:c + 1], rs[k2][:], ALU.mult, ALU.mult),
                            reads=[cfb[k2], rsb[k2], tb], writes=[cnb[k2]])
                    if part == 0:
                        P.dma("sp", scr["cq"][:, ti * 512:(ti + 1) * 512].rearrange("(c p) t -> p c t", p=128),
                              cn[k2][:], reads=[cnb[k2]])
                    else:
                        P.dma("sp", scr["lat"][half][0:512, tcol:tcol + 512].rearrange("(c p) t -> p c t", p=128),
                              cn[k2][:], reads=[cnb[k2]])
                pp = []
                for x_ in range(2):
                    pt, pbuf = self.psum()
                    col = 1024 + x_ * 32
                    for kc in range(DC):
                        P.op("pe", lambda E, pt=pt, kc=kc, col=col, ti=ti: E.matmul(
                            pt[0:32, :], wt[:, kc, col:col + 32], aT[:, kc, ti * 512:(ti + 1) * 512],
                            start=(kc == 0), stop=(kc == DC - 1)),
                            reads=[wb, abuf], writes=[pbuf], inc=(kc == DC - 1))
                    pp.append((pt, pbuf))
                self.rope32(pp, mcos[:, ti * 512:(ti + 1) * 512], msin[:, ti * 512:(ti + 1) * 512], tb, tmp, tmpb,
                            kr[ti % 2], krb[ti % 2], 1.0)
                P.dma("sp", scr["lat"][half][512:576, tcol:tcol + 512].rearrange("(x p) t -> p x t", p=32),
                      kr[ti % 2][:], reads=[krb[ti % 2]])
            self.end_stage()
            for i in range(2):
                P.allgather(scr["lat"][i], scr["lout"][i])
            self.end_stage()

        with contextlib.ExitStack() as st:
            ckv = self.sb(st, "mckv", [128, 4, SEQ], BF16)
            k12 = [self.sb(st, "mk%d" % i, [32, SEQ], BF16) for i in range(2)]
            cq = self.sb(st, "mcq", [128, 4, TL], BF16)
            wuq = self.sb(st, "mwuq", [128, 4, 3072], BF16)
            wukv = self.sb(st, "mwukv", [128, 4, 4096], BF16)
            mcos = self.sb(st, "mcos2", [32, TL], F32)
            msin = self.sb(st, "msin2", [32, TL], F32)
            lb = Buf()
            tb = Buf()
            wb = Buf()
            P.dma("sp", mcos[:], io["mcos"], writes=[tb])
            P.dma("sp", msin[:], io["msin"], writes=[tb])
            for r_ in range(2):
                for i in range(2):
                    blk = (r_ * 2 + i) * 1024
                    src = scr["lout"][i][r_ * 576:(r_ + 1) * 576, :]
                    P.dma("sp", ckv[:, :, blk:blk + 1024], src[0:512, :].rearrange("(c p) t -> p c t", p=128), writes=[lb])
                    P.dma("sp", k12[0][:, blk:blk + 1024], src[512:544, :], writes=[lb])
                    P.dma("sp", k12[1][:, blk:blk + 1024], src[544:576, :], writes=[lb])
            P.dma("sp", cq[:], scr["cq"].rearrange("(c p) t -> p c t", p=128), writes=[lb])
            for c0 in range(0, 3072, 1024):
                P.dma("pool", wuq[:, :, c0:c0 + 1024], W_uq[:, c0:c0 + 1024].rearrange("(c p) n -> p c n", p=128), writes=[wb])
            for c0 in range(0, 4096, 1024):
                P.dma("pool", wukv[:, :, c0:c0 + 1024], W_ukv[:, c0:c0 + 1024].rearrange("(c p) n -> p c n", p=128), writes=[wb])
            qn = self.sb(st, "mqn", [128, TL], BF16)
            q12 = self.sb(st, "mq12", [32, 2, TL], BF16)
            kn = self.sb(st, "mkn", [128, SEQ], BF16)
            vh = self.sb(st, "mvh", [128, 32, 128], BF16)
            qb, knb, vhb = Buf(), Buf(), Buf()
            q12b = Buf()
            tmp = [self.sb(st, "matmp%d" % i, [32, 512], F32) for i in range(4)]
            tmpb = [Buf() for _ in range(4)]
            pT = [self.sb(st, "mpT%d" % i, [128, 512], BF16) for i in range(3)]
            pTb = [Buf() for _ in range(3)]
            rec = self.sb(st, "mrec", [128, 512], F32)
            recb = Buf()
            oT = [self.sb(st, "moT%d" % i, [128, TL], BF16) for i in range(2)]
            oTb = [Buf() for _ in range(2)]
            scale = 192.0 ** -0.5
            pit = 0
            for h in range(16):
                for ti in range(NT):
                    pt, pbuf = self.psum4()
                    for kc in range(4):
                        P.op("pe", lambda E, pt=pt, kc=kc, h=h, ti=ti: E.matmul(
                            pt[:, :], wuq[:, kc, h * 192:h * 192 + 128], cq[:, kc, ti * 512:(ti + 1) * 512],
                            start=(kc == 0), stop=(kc == 3)),
                            reads=[wb, lb], writes=[pbuf], inc=(kc == 3))
                    P.op("act", lambda E, pt=pt, ti=ti: E.activation(qn[:, ti * 512:(ti + 1) * 512], pt[:, :], AF.Copy,
                                                                     scale=scale),
                         reads=[pbuf], writes=[qb])
                    pp = []
                    for x_ in range(2):
                        pt2, pbuf2 = self.psum4()
                        col = h * 192 + 128 + x_ * 32
                        for kc in range(4):
                            P.op("pe", lambda E, pt2=pt2, kc=kc, col=col, ti=ti: E.matmul(
                                pt2[0:32, :], wuq[:, kc, col:col + 32], cq[:, kc, ti * 512:(ti + 1) * 512],
                                start=(kc == 0), stop=(kc == 3)),
                                reads=[wb, lb], writes=[pbuf2], inc=(kc == 3))
                        pp.append((pt2, pbuf2))
                    self.rope32(pp, mcos[:, ti * 512:(ti + 1) * 512], msin[:, ti * 512:(ti + 1) * 512], tb, tmp, tmpb,
                                q12[:, :, ti * 512:(ti + 1) * 512], q12b, scale)
                for kt in range(SEQ // 512):
                    pt, pbuf = self.psum4()
                    for kc in range(4):
                        P.op("pe", lambda E, pt=pt, kc=kc, h=h, kt=kt: E.matmul(
                            pt[:, :], wukv[:, kc, h * 256:h * 256 + 128], ckv[:, kc, kt * 512:(kt + 1) * 512],
                            start=(kc == 0), stop=(kc == 3)),
                            reads=[wb, lb], writes=[pbuf], inc=(kc == 3))
                    P.op("act", lambda E, pt=pt, kt=kt: E.copy(kn[:, kt * 512:(kt + 1) * 512], pt[:, :]),
                         reads=[pbuf], writes=[knb])
                    pt, pbuf = self.psum4()
                    for j in range(4):
                        kc32 = kt * 4 + j
                        for kc in range(4):
                            P.op("pe", lambda E, pt=pt, kc=kc, h=h, kc32=kc32, j=j: E.matmul(
                                pt[:, j * 128:(j + 1) * 128], ckv[:, kc, kc32 * 128:(kc32 + 1) * 128],
                                wukv[:, kc, h * 256 + 128:h * 256 + 256], start=(kc == 0), stop=(kc == 3)),
                                reads=[wb, lb], writes=[pbuf], inc=(kc == 3))
                    P.op("act", lambda E, pt=pt, kt=kt: E.copy(
                        vh[:, kt * 4:(kt + 1) * 4, :].rearrange("p a b -> p (a b)"), pt[:, :]),
                        reads=[pbuf], writes=[vhb])
                o_, ob_ = oT[h % 2], oTb[h % 2]
                pO, pOb = self.ps[4], self.psb[4]
                pZ, pZb = self.ps[5], self.psb[5]
                for qt in range(NT):
                    for kc32 in range(32):
                        pS, pSb = self.psum4()
                        ks = slice(kc32 * 128, (kc32 + 1) * 128)
                        P.op("pe", lambda E, pS=pS, ks=ks, qt=qt: E.matmul(
                            pS[:, :], kn[:, ks], qn[:, qt * 512:(qt + 1) * 512], start=True, stop=False),
                            reads=[knb, qb], writes=[pSb], inc=False)
                        for x_ in range(2):
                            P.op("pe", lambda E, pS=pS, ks=ks, qt=qt, x_=x_: E.matmul(
                                pS[:, :], k12[x_][:, ks], q12[:, x_, qt * 512:(qt + 1) * 512],
                                start=False, stop=(x_ == 1)),
                                reads=[lb, q12b], writes=[pSb], inc=(x_ == 1))
                        k3 = pit % 3
                        pit += 1
                        P.op("act", lambda E, k3=k3, pS=pS: E.activation(pT[k3][:], pS[:, :], AF.Exp),
                             reads=[pSb], writes=[pTb[k3]])
                        P.op("pe", lambda E, pO=pO, kc32=kc32, k3=k3: E.matmul(
                            pO[:, :], vh[:, kc32, :], pT[k3][:], start=(kc32 == 0), stop=(kc32 == 31),
                            skip_group_check=True),
                            reads=[vhb, pTb[k3]], writes=[pOb], inc=(kc32 == 31))
                        P.op("pe", lambda E, pZ=pZ, kc32=kc32, k3=k3: E.matmul(
                            pZ[:, :], cs["onesb"][:], pT[k3][:], start=(kc32 == 0), stop=(kc32 == 31),
                            skip_group_check=True),
                            reads=[self.cbuf, pTb[k3]], writes=[pZb], inc=(kc32 == 31))
                    P.op("dve", lambda E, pZ=pZ: E.reciprocal(rec[:], pZ[:, :]), reads=[pZb], writes=[recb])
                    P.op("dve", lambda E, o_=o_, pO=pO, qt=qt: E.tensor_tensor(
                        o_[:, qt * 512:(qt + 1) * 512], pO[:, :], rec[:], ALU.mult),
                        reads=[pOb, recb], writes=[ob_])
                P.dma("sp", scr["oT"][h * 128:(h + 1) * 128, :], o_[:], reads=[ob_])
            self.end_stage()
        self.outproj_stage(W_out, scr["oT"], None, D)

    def psum4(self):
        i = self.psi % 4
        self.psi = (self.psi + 1) % 4
        return self.ps[i], self.psb[i]

    def rope32(self, pp, cosv, sinv, tb, tmp, tmpb, out, outb, scale):
        P = self.P
        (p0, b0), (p1, b1) = pp
        P.op("dve", lambda E: E.scalar_tensor_tensor(tmp[0][:], p0[0:32, :], scale, cosv, ALU.mult, ALU.mult),
             reads=[b0, tb], writes=[tmpb[0]])
        P.op("dve", lambda E: E.scalar_tensor_tensor(tmp[1][:], p1[0:32, :], scale, sinv, ALU.mult, ALU.mult),
             reads=[b1, tb], writes=[tmpb[1]])
        P.op("dve", lambda E: E.scalar_tensor_tensor(tmp[2][:], p0[0:32, :], scale, sinv, ALU.mult, ALU.mult),
             reads=[b0, tb], writes=[tmpb[2]])
        P.op("dve", lambda E: E.scalar_tensor_tensor(tmp[3][:], p1[0:32, :], scale, cosv, ALU.mult, ALU.mult),
             reads=[b1, tb], writes=[tmpb[3]])
        P.op("pool", lambda E: E.tensor_tensor(out[:, 0, :], tmp[0][:], tmp[1][:], ALU.subtract),
             reads=[tmpb[0], tmpb[1]], writes=[outb])
        P.op("pool", lambda E: E.tensor_tensor(out[:, 1, :], tmp[2][:], tmp[3][:], ALU.add),
             reads=[tmpb[2], tmpb[3]], writes=[outb])

    def final_norm(self):
        P = self.P
        with contextlib.ExitStack() as st:
            cs = self.consts
            hts = [self.sb(st, "fnh%d" % i, [128, DC, 512], F32) for i in range(2)]
            hbs = [Buf() for _ in range(2)]
            sqs = [self.sb(st, "fnsq%d" % i, [128, 512], F32) for i in range(4)]
            sqb = [Buf() for _ in range(4)]
            rstd = [self.sb(st, "fnr%d" % i, [128, 512], F32) for i in range(2)]
            rb = [Buf() for _ in range(2)]
            hview = self.hT.rearrange("(c p) t -> p c t", p=128)
            oview = self.io["outT"].rearrange("(c p) t -> p c t", p=128)
            toks = []
            for ti in range(NT):
                ht, hb = hts[ti % 2], hbs[ti % 2]
                for c8 in range(2):
                    P.dma("sp", ht[:, c8 * 8:(c8 + 1) * 8, :], hview[:, c8 * 8:(c8 + 1) * 8, ti * 512:(ti + 1) * 512], writes=[hb])
                pt, pbuf = self.psum()
                for c in range(DC):
                    sq, sb_ = sqs[c % 4], sqb[c % 4]
                    P.op("act", lambda E, sq=sq, ht=ht, c=c: E.activation(sq[:], ht[:, c, :], AF.Square),
                         reads=[hb], writes=[sb_])
                    P.op("pe", lambda E, pt=pt, sq=sq, c=c: E.matmul(pt[:, :], cs["ones"][:, :], sq[:, :],
                                                                     start=(c == 0), stop=(c == DC - 1)),
                         reads=[sb_, self.cbuf], writes=[pbuf], inc=True)
                rs, rbuf = rstd[ti % 2], rb[ti % 2]
                P.op("act", lambda E, rs=rs, pt=pt: E.activation(rs[:], pt[:, :], AF.Sqrt, scale=1.0 / D,
                                                                 bias=cs["eps6"][:, 0:1]),
                     reads=[pbuf, self.cbuf], writes=[rbuf])
                P.op("dve", lambda E, rs=rs: E.reciprocal(rs[:], rs[:]), reads=[rbuf], writes=[rbuf])
                for c in range(DC):
                    eng = "dve"
                    P.op(eng, lambda E, c=c, ht=ht, rs=rs: E.scalar_tensor_tensor(
                        ht[:, c, :], ht[:, c, :], cs["gains"][:, 8, c:c + 1], rs[:], ALU.mult, ALU.mult),
                        reads=[hb, rbuf, self.cbuf], writes=[hb])
                for c8 in range(2):
                    toks.append(P.dma("sp", oview[:, c8 * 8:(c8 + 1) * 8, ti * 512:(ti + 1) * 512], ht[:, c8 * 8:(c8 + 1) * 8, :], reads=[hb]))
            for t in toks:
                P._wait("sp", t)
            self.end_stage()


def weight_shapes(cfg):
    nl = cfg.get("nlayers", 4)
    stop = cfg.get("stop")
    ws = {}
    for li in range(nl):
        kind = li % 3
        if kind == 0:
            ws["l%d_ret_w_in" % li] = (D, 12288)
            ws["l%d_ret_w_out" % li] = (4096, D)
        elif kind == 1:
            ws["l%d_na_w_qkv" % li] = (D, 6144)
            ws["l%d_na_w_out" % li] = (D, D)
        else:
            ws["l%d_mla_w_down" % li] = (D, 1088)
            ws["l%d_mla_w_uq" % li] = (512, 3072)
            ws["l%d_mla_w_ukv" % li] = (512, 4096)
            ws["l%d_mla_w_out" % li] = (D, D)
        if stop == "mix%d" % li:
            break
        ws["l%d_ffn_w_up" % li] = (D, 2 * DFF)
        ws["l%d_ffn_w_down" % li] = (DFF, D)
    return ws


def piece_rows(K_, N_, nco):
    rows = K_ // nco
    pr = rows
    while pr * N_ * 4 > (1 << 20) or rows % pr != 0:
        pr -= 1
    return pr


def na_bias_table(rpb, r):
    import ml_dtypes
    TE = TL + 256
    e = np.arange(TE)
    if r == 0:
        grow = np.where(e < TL, e // 64, 32 + (e - TL) // 64)
        gcol = np.where(e < TL, e % 64, 63 - (e - TL) % 64)
    else:
        grow = np.where(e < TL, 63 - e // 64, 31 - (e - TL) // 64)
        gcol = np.where(e < TL, 63 - e % 64, (e - TL) % 64)
    NEG = np.float32(-30000.0)
    rpb_ext = np.concatenate([rpb.reshape(16, -1), np.full((16, 1), NEG, np.float32)], axis=1)
    out = np.zeros((16, 25, 128, 128), np.float32)
    for pi, p in enumerate((0, 1, 2, 14, 15)):
        start = max(2 * p - 4, 0)
        q = 128 * p + np.arange(128)
        qr, qc = grow[q], gcol[q]
        rs = np.clip(qr - 4, 0, 56)
        cs_ = np.clip(qc - 8, 0, 48)
        for c in range(5):
            k = 64 * (start + 2 * c) + np.arange(128)
            kr, kc = grow[k], gcol[k]
            valid = ((kr[:, None] >= rs[None, :]) & (kr[:, None] < rs[None, :] + 8) &
                     (kc[:, None] >= cs_[None, :]) & (kc[:, None] < cs_[None, :] + 16))
            dr = np.clip(kr[:, None] - qr[None, :] + 7, 0, 14)
            dcx = np.clip(kc[:, None] - qc[None, :] + 15, 0, 30)
            flat = np.where(valid, dr * 31 + dcx, 15 * 31)
            out[:, pi * 5 + c] = rpb_ext[:, flat]
    return out.astype(ml_dtypes.bfloat16)


def _chunked(vec, nch):
    return np.ascontiguousarray(np.asarray(vec, np.float32).reshape(nch, 128).T)


def prep_inputs(inputs, cfg):
    f32 = np.float32
    shared = {}
    shared["ident"] = np.eye(128, dtype=f32)
    gn = ["l0_attn_norm", "l1_attn_norm", "l2_attn_norm", "l3_attn_norm",
          "l0_ffn_norm", "l1_ffn_norm", "l2_ffn_norm", "l3_ffn_norm", "final_norm"]
    shared["gains"] = np.ascontiguousarray(np.stack([_chunked(inputs[n], DC) for n in gn], axis=1))
    nco = cfg.get("ncores", 8)
    wsh = weight_shapes(cfg)
    ii = np.arange(128, dtype=f32)
    dA = np.maximum(ii[None, :] - ii[:, None], 0.0)
    dB = np.maximum(ii[:, None] - ii[None, :], 0.0)
    qiA = np.tile((np.arange(512) % 128 + 1).astype(f32)[None, :], (128, 1))
    qiB = np.tile((128 - np.arange(512) % 128).astype(f32)[None, :], (128, 1))
    shared["qi"] = np.ascontiguousarray(np.stack([qiA, qiB], axis=1))
    shared["ki"] = np.ascontiguousarray(np.stack([127.0 - ii, ii], axis=1).astype(f32))
    x = np.asarray(inputs["x"], f32)
    x_ = x
    per_core = []
    inv = (10000.0 ** (-np.arange(128, dtype=f32) * 2.0 / 256)).astype(f32)
    for c in range(8):
        b, r = c // 2, c % 2
        idx = np.arange(TL) if r == 0 else (SEQ - 1 - np.arange(TL))
        m = dict(shared)
        m["xT"] = np.ascontiguousarray(x_[b][idx, :].T)
        m["m01"] = np.tile(np.array([[1.0, 0.0]] if r == 1 else [[0.0, 1.0]], f32), (128, 1))
        pos = idx.astype(f32)
        ang = (pos[:, None] * inv[None, :]).astype(f32)
        m["rcos"] = np.ascontiguousarray(np.cos(ang).T.astype(f32))
        m["rsin"] = np.ascontiguousarray(np.sin(ang).T.astype(f32))
        if r == 0:
            mA = (ii[None, :] >= ii[:, None]).astype(f32)
            mB = (ii[:, None] > ii[None, :]).astype(f32)
        else:
            mA = (ii[None, :] > ii[:, None]).astype(f32)
            mB = (ii[:, None] >= ii[None, :]).astype(f32)
        m["rtabs"] = np.ascontiguousarray(np.stack([dA, dB, mA, mB], axis=1))
        rdec = np.zeros((128, 2, 16), f32)
        for k, li in enumerate((0, 3)):
            df = np.asarray(inputs["l%d_ret_decay_fwd" % li], f32)
            db = np.asarray(inputs["l%d_ret_decay_bwd" % li], f32)
            A, B = (df, db) if r == 0 else (db, df)
            rdec[:, k, 0:8] = A[None, :]
            rdec[:, k, 8:16] = B[None, :]
        m["rdec"] = rdec
        cp = np.zeros((128, 4, 2 * FC, 4), f32)
        for li in range(4):
            cw = np.asarray(inputs["l%d_ffn_conv_w" % li], f32)
            cb = np.asarray(inputs["l%d_ffn_conv_b" % li], f32)
            w0, w1, w2 = (cw[0], cw[1], cw[2]) if r == 0 else (cw[2], cw[1], cw[0])
            for k, vec in enumerate((w0, w1, w2, cb)):
                cp[:, li, :, k] = vec.reshape(2 * FC, 128).T
        m["convp"] = cp
        if cfg.get("nlayers", 4) > 1 and c < nco:
            m["na_bias"] = na_bias_table(np.asarray(inputs["l1_na_rpb"], f32), r)
        if cfg.get("nlayers", 4) > 2 and c < nco:
            inv32 = (10000.0 ** (-np.arange(32, dtype=f32) * 2.0 / 64)).astype(f32)
            ang32 = (pos[:, None] * inv32[None, :]).astype(f32)
            m["mcos"] = np.ascontiguousarray(np.cos(ang32).T.astype(f32))
            m["msin"] = np.ascontiguousarray(np.sin(ang32).T.astype(f32))
            m["mla_gains"] = np.ascontiguousarray(np.stack(
                [_chunked(inputs["l2_mla_q_norm"], 4), _chunked(inputs["l2_mla_kv_norm"], 4)], axis=1))
        if c < nco:
            for name, (K_, N_) in wsh.items():
                pr = piece_rows(K_, N_, nco)
                w = np.asarray(inputs[name], f32).reshape(K_ // (nco * pr), nco, pr, N_)
                m["w_" + name] = np.ascontiguousarray(w[:, c].reshape(K_ // nco, N_))
        per_core.append(m)
    return per_core


_CACHE = {}


def run(inputs, cfg):
    key = tuple(sorted(cfg.items()))
    if key not in _CACHE:
        kb = KB(cfg)
        _CACHE[key] = (kb, kb.build())
    kb, nc = _CACHE[key]
    in_maps = prep_inputs(inputs, cfg)
    names = set(kb.ext_names)
    in_maps = [{k: v for k, v in m.items() if k in names} for m in in_maps]
    ncores = cfg.get("ncores", 8)
    res = run_bass_kernel_spmd(nc, in_maps[:ncores], core_ids=list(range(ncores)))
    return res


def assemble(res, ncores=8):
    out = np.zeros((4, SEQ, D), np.float32)
    for c in range(ncores):
        b, r = c // 2, c % 2
        o = res.results[c]["outT"].T
        if r == 0:
            out[b, :TL] = o
        else:
            out[b, TL:] = o[::-1]
    return out


def kernel(**inputs):
    cfg = {}
    res = run(inputs, cfg)
    return assemble(res)
```
